# Optimizing a Trainium2 kernel written in Bass

```python
import math
import jax, jax.numpy as jnp
from jax import lax
import numpy as np

D_MODEL = 2048
BATCH = 16
SEQ = 2048
DEPTH = 1

N_META = 16
D_MIX = D_MODEL
ATTN_WIDTH = D_MIX // 2
MLSTM_WIDTH = D_MIX - ATTN_WIDTH
ATTN_HEAD_DIM = 64
ATTN_HEADS = ATTN_WIDTH // (2 * ATTN_HEAD_DIM)
ATTN_V_DIM = 2 * ATTN_HEAD_DIM
ROPE_THETA = 10000.0
Q_BLOCK = 128
PAD_LEN = Q_BLOCK - N_META
MLSTM_HEADS = 4
MLSTM_V_DIM = MLSTM_WIDTH // MLSTM_HEADS
MLSTM_QK_DIM = MLSTM_V_DIM // 2
MLSTM_CHUNK = 64
CONV_WIDTH = 4
FORGET_BIAS_INIT = 3.0
PEER_HEADS = 8
PEER_N_KEYS = 128
PEER_N_EXPERTS = PEER_N_KEYS * PEER_N_KEYS
PEER_SUB_DIM = 128
PEER_KEY_DIM = 2 * PEER_SUB_DIM
PEER_TOPK = 16
PEER_TOK_BLOCK = 128
EPS = 1e-6
NEG = -1e30

IN_SIZES = (
    ATTN_HEADS * 2 * ATTN_HEAD_DIM,
    ATTN_HEADS * 2 * ATTN_HEAD_DIM,
    ATTN_HEADS * ATTN_V_DIM,
    MLSTM_HEADS * MLSTM_QK_DIM,
    MLSTM_HEADS * MLSTM_QK_DIM,
    MLSTM_HEADS * MLSTM_V_DIM,
    MLSTM_HEADS * MLSTM_V_DIM,
    MLSTM_HEADS,
    MLSTM_HEADS,
)
IN_COLS = sum(IN_SIZES)

kernel_name = 'hybrid_diffattn_mlstm_peer'


def _split_points():
    pts, acc = [], 0
    for s in IN_SIZES[:-1]:
        acc += s
        pts.append(acc)
    return pts


def rmsnorm(x, w):
    xf = x.astype(jnp.float32)
    y = xf * lax.rsqrt(jnp.mean(xf * xf, axis=-1, keepdims=True) + EPS)
    return (y * w.astype(jnp.float32)).astype(x.dtype)


def rope_tables(pos, dim):
    inv_freq = ROPE_THETA ** (-jnp.arange(0, dim, 2, dtype=jnp.float32) / dim)
    ang = pos.astype(jnp.float32)[:, None] * inv_freq[None, :]
    ang = jnp.concatenate([ang, ang], axis=-1)
    return jnp.cos(ang), jnp.sin(ang)


def apply_rope(x, cos, sin):
    c = cos.astype(x.dtype)[None, :, None, None, :]
    s = sin.astype(x.dtype)[None, :, None, None, :]
    x1, x2 = jnp.split(x, 2, axis=-1)
    return x * c + jnp.concatenate([-x2, x1], axis=-1) * s


def causal_conv(x, w, b):
    c = x.shape[-1]
    y = lax.conv_general_dilated(x, w[:, None, :], window_strides=(1,),
                                 padding=[(CONV_WIDTH - 1, 0)],
                                 dimension_numbers=('NWC', 'WIO', 'NWC'),
                                 feature_group_count=c)
    return y + b


def diff_attention(q, k, v, lam_qk, subln_w, lambda_init):
    B, P = q.shape[0], q.shape[1]
    nb = P // Q_BLOCK
    pos = jnp.maximum(jnp.arange(P) - PAD_LEN, 0)
    cos, sin = rope_tables(pos, ATTN_HEAD_DIM)
    q = apply_rope(q, cos, sin) * (ATTN_HEAD_DIM ** -0.5)
    k = apply_rope(k, cos, sin)
    lq = lam_qk.astype(jnp.float32)
    lam = jnp.exp(jnp.sum(lq[0] * lq[1])) - jnp.exp(jnp.sum(lq[2] * lq[3])) + lambda_init
    q_blocks = q.reshape(B, nb, Q_BLOCK, ATTN_HEADS, 2, ATTN_HEAD_DIM).transpose(1, 0, 2, 3, 4, 5)
    key_idx = jnp.arange(P)

    def block(args):
        qb, bi = args
        q_idx = bi * Q_BLOCK + jnp.arange(Q_BLOCK)
        s = jnp.einsum('bqhcd,bkhcd->bhcqk', qb, k).astype(jnp.float32)
        mask = (key_idx[None, :] <= q_idx[:, None]) & (key_idx[None, :] >= PAD_LEN)
        s = jnp.where(mask[None, None, None], s, NEG)
        p = jax.nn.softmax(s, axis=-1)
        a = p[:, :, 0] - lam * p[:, :, 1]
        return jnp.einsum('bhqk,bkhe->bqhe', a.astype(v.dtype), v)

    out = lax.map(block, (q_blocks, jnp.arange(nb)))
    out = out.transpose(1, 0, 2, 3, 4).reshape(B, P, ATTN_HEADS, ATTN_V_DIM)
    out = rmsnorm(out, subln_w) * (1.0 - lambda_init)
    return out.reshape(B, P, ATTN_HEADS * ATTN_V_DIM)


def mlstm_chunkwise(q, k, v, log_i, log_f):
    B, P = q.shape[0], q.shape[1]
    H, CL = MLSTM_HEADS, MLSTM_CHUNK
    nc = P // CL

    def to_chunks(t):
        return t.astype(jnp.float32).reshape(B, nc, CL, H, -1).transpose(0, 3, 1, 2, 4)

    qc = to_chunks(q) * (MLSTM_QK_DIM ** -0.5)
    kc, vc = to_chunks(k), to_chunks(v)
    li = log_i.reshape(B, nc, CL, H).transpose(0, 3, 1, 2)
    lf = log_f.reshape(B, nc, CL, H).transpose(0, 3, 1, 2)
    b = jnp.cumsum(lf, axis=-1)
    g = b[..., -1]
    a = g[..., None] - b + li

    def step(carry, xs):
        C, n, m = carry
        k_s, v_s, a_s, g_s = xs
        m_new = jnp.maximum(g_s + m, jnp.max(a_s, axis=-1))
        w = jnp.exp(a_s - m_new[..., None])
        decay = jnp.exp(g_s + m - m_new)
        C_new = decay[..., None, None] * C + jnp.einsum('bhs,bhsk,bhsv->bhkv', w, k_s, v_s)
        n_new = decay[..., None] * n + jnp.einsum('bhs,bhsk->bhk', w, k_s)
        return (C_new, n_new, m_new), (C, n, m)

    init = (jnp.zeros((B, H, MLSTM_QK_DIM, MLSTM_V_DIM), jnp.float32),
            jnp.zeros((B, H, MLSTM_QK_DIM), jnp.float32),
            jnp.full((B, H), NEG, jnp.float32))
    xs = (kc.transpose(2, 0, 1, 3, 4), vc.transpose(2, 0, 1, 3, 4),
          a.transpose(2, 0, 1, 3), g.transpose(2, 0, 1))
    _, (C_prev, n_prev, m_prev) = lax.scan(step, init, xs)
    C_prev = C_prev.transpose(1, 2, 0, 3, 4)
    n_prev = n_prev.transpose(1, 2, 0, 3)
    m_prev = m_prev.transpose(1, 2, 0)

    causal = jnp.tril(jnp.ones((CL, CL), dtype=bool))
    D = jnp.where(causal, b[..., :, None] - b[..., None, :] + li[..., None, :], NEG)
    m_inter = b + m_prev[..., None]
    m_t = jnp.maximum(jnp.max(D, axis=-1), m_inter)
    S = jnp.einsum('bhctd,bhcsd->bhcts', qc, kc) * jnp.exp(D - m_t[..., None])
    inter_w = jnp.exp(m_inter - m_t)
    num = jnp.einsum('bhcts,bhcsv->bhctv', S, vc) + \
        inter_w[..., None] * jnp.einsum('bhctk,bhckv->bhctv', qc, C_prev)
    nq = jnp.sum(S, axis=-1) + inter_w * jnp.einsum('bhctk,bhck->bhct', qc, n_prev)
    h = num / jnp.maximum(jnp.abs(nq), jnp.exp(-m_t))[..., None]
    return h.transpose(0, 2, 3, 1, 4).reshape(B, P, H, MLSTM_V_DIM)


def peer_ffn(h, w_q, sub_keys, u, v):
    B, L, D = h.shape
    T = B * L
    ht = h.reshape(T, D)
    q = (ht @ w_q).reshape(T, PEER_HEADS, 2, PEER_SUB_DIM)
    s = jnp.einsum('thpd,hpnd->thpn', q, sub_keys).astype(jnp.float32)
    s1, i1 = lax.top_k(s[:, :, 0], PEER_TOPK)
    s2, i2 = lax.top_k(s[:, :, 1], PEER_TOPK)
    cand_s = (s1[..., :, None] + s2[..., None, :]).reshape(T, PEER_HEADS, PEER_TOPK * PEER_TOPK)
    cand_i = (i1[..., :, None] * PEER_N_KEYS + i2[..., None, :]).reshape(T, PEER_HEADS, PEER_TOPK * PEER_TOPK)
    best_s, best_pos = lax.top_k(cand_s, PEER_TOPK)
    idx = jnp.take_along_axis(cand_i, best_pos, axis=-1).reshape(T, PEER_HEADS * PEER_TOPK)
    gate = jax.nn.softmax(best_s, axis=-1).reshape(T, PEER_HEADS * PEER_TOPK).astype(h.dtype)
    nblk = -(-T // PEER_TOK_BLOCK)
    pad = nblk * PEER_TOK_BLOCK - T
    ht_b = jnp.pad(ht, ((0, pad), (0, 0))).reshape(nblk, PEER_TOK_BLOCK, D)
    idx_b = jnp.pad(idx, ((0, pad), (0, 0))).reshape(nblk, PEER_TOK_BLOCK, -1)
    g_b = jnp.pad(gate, ((0, pad), (0, 0))).reshape(nblk, PEER_TOK_BLOCK, -1)

    def block(args):
        hb, ib, gb = args
        ub = jnp.take(u, ib, axis=0)
        act = jax.nn.gelu(jnp.einsum('td,tkd->tk', hb, ub), approximate=False)
        return jnp.einsum('tk,tkd->td', gb * act, jnp.take(v, ib, axis=0))

    y = lax.map(block, (ht_b, idx_b, g_b)).reshape(nblk * PEER_TOK_BLOCK, D)[:T]
    return y.reshape(B, L, D)


def hybrid_layer(x, norm_mix_w, w_in, attn_lambda_qk, attn_subln_w, conv_w, conv_b,
                 i_b, f_b, mlstm_norm_w, w_out, norm_ffn_w, peer_w_q, peer_sub_keys,
                 peer_u, peer_v, lambda_init):
    B, L, _ = x.shape
    P = L + PAD_LEN
    h = rmsnorm(x, norm_mix_w)
    z = jnp.einsum('bld,dc->blc', h, w_in)
    z = jnp.pad(z, ((0, 0), (PAD_LEN, 0), (0, 0)))
    aq, ak, av, mq, mk, mv, mo, mi, mf = jnp.split(z, _split_points(), axis=-1)
    valid = (jnp.arange(P) >= PAD_LEN)[None, :, None]

    attn_out = diff_attention(aq.reshape(B, P, ATTN_HEADS, 2, ATTN_HEAD_DIM),
                              ak.reshape(B, P, ATTN_HEADS, 2, ATTN_HEAD_DIM),
                              av.reshape(B, P, ATTN_HEADS, ATTN_V_DIM),
                              attn_lambda_qk, attn_subln_w, lambda_init)

    qk = jax.nn.silu(causal_conv(jnp.concatenate([mq, mk], axis=-1), conv_w, conv_b))
    mq, mk = jnp.split(qk, 2, axis=-1)
    log_i = jnp.where(valid, mi.astype(jnp.float32) + i_b.astype(jnp.float32), NEG)
    log_f = jnp.where(valid, jax.nn.log_sigmoid(mf.astype(jnp.float32) + f_b.astype(jnp.float32)), 0.0)
    hm = mlstm_chunkwise(mq.reshape(B, P, MLSTM_HEADS, MLSTM_QK_DIM),
                         mk.reshape(B, P, MLSTM_HEADS, MLSTM_QK_DIM),
                         mv.reshape(B, P, MLSTM_HEADS, MLSTM_V_DIM), log_i, log_f)
    hm = rmsnorm(hm, mlstm_norm_w.reshape(MLSTM_HEADS, MLSTM_V_DIM)).astype(x.dtype)
    mlstm_out = (hm * jax.nn.sigmoid(mo.reshape(B, P, MLSTM_HEADS, MLSTM_V_DIM))).reshape(B, P, MLSTM_WIDTH)

    mix = jnp.concatenate([attn_out, mlstm_out], axis=-1)[:, PAD_LEN:]
    x = x + jnp.einsum('blc,cd->bld', mix, w_out)
    x = x + peer_ffn(rmsnorm(x, norm_ffn_w), peer_w_q, peer_sub_keys, peer_u, peer_v)
    return x


def setup_inputs(seed: int = 0) -> dict:
    key = jax.random.key(seed)
    ks = jax.random.split(key, 18)
    f32 = jnp.float32
    nrm = lambda k, shape, scale: scale * jax.random.normal(k, shape, f32)
    return {
        'x': nrm(ks[0], (BATCH, SEQ, D_MODEL), 1.0),
        'meta_tokens': nrm(ks[1], (N_META, D_MODEL), 1.0),
        'norm_mix_w': 1.0 + nrm(ks[2], (DEPTH, D_MODEL), 0.02),
        'w_in': nrm(ks[3], (DEPTH, D_MODEL, IN_COLS), D_MODEL ** -0.5),
        'attn_lambda_qk': nrm(ks[4], (DEPTH, 4, ATTN_HEAD_DIM), 0.1),
        'attn_subln_w': 1.0 + nrm(ks[5], (DEPTH, ATTN_V_DIM), 0.02),
        'mlstm_conv_w': nrm(ks[6], (DEPTH, CONV_WIDTH, 2 * MLSTM_HEADS * MLSTM_QK_DIM), CONV_WIDTH ** -0.5),
        'mlstm_conv_b': nrm(ks[7], (DEPTH, 2 * MLSTM_HEADS * MLSTM_QK_DIM), 0.02),
        'mlstm_i_b': nrm(ks[8], (DEPTH, MLSTM_HEADS), 0.1),
        'mlstm_f_b': FORGET_BIAS_INIT + nrm(ks[9], (DEPTH, MLSTM_HEADS), 0.5),
        'mlstm_norm_w': 1.0 + nrm(ks[10], (DEPTH, MLSTM_WIDTH), 0.02),
        'w_out': nrm(ks[11], (DEPTH, D_MIX, D_MODEL), D_MIX ** -0.5),
        'norm_ffn_w': 1.0 + nrm(ks[12], (DEPTH, D_MODEL), 0.02),
        'peer_w_q': nrm(ks[13], (DEPTH, D_MODEL, PEER_HEADS * PEER_KEY_DIM), D_MODEL ** -0.5),
        'peer_sub_keys': nrm(ks[14], (DEPTH, PEER_HEADS, 2, PEER_N_KEYS, PEER_SUB_DIM), PEER_SUB_DIM ** -0.5),
        'peer_u': nrm(ks[15], (DEPTH, PEER_N_EXPERTS, D_MODEL), D_MODEL ** -0.5),
        'peer_v': nrm(ks[16], (DEPTH, PEER_N_EXPERTS, D_MODEL), D_MODEL ** -0.5),
        'norm_final_w': 1.0 + nrm(ks[17], (D_MODEL,), 0.02),
    }


def reference(x, meta_tokens, norm_mix_w, w_in, attn_lambda_qk, attn_subln_w, mlstm_conv_w,
              mlstm_conv_b, mlstm_i_b, mlstm_f_b, mlstm_norm_w, w_out, norm_ffn_w, peer_w_q,
              peer_sub_keys, peer_u, peer_v, norm_final_w):
    B = x.shape[0]
    meta = jnp.broadcast_to(meta_tokens[None].astype(x.dtype), (B, N_META, D_MODEL))
    h = jnp.concatenate([meta, x], axis=1)
    for layer in range(DEPTH):
        lambda_init = 0.8 - 0.6 * math.exp(-0.3 * layer)
        h = hybrid_layer(h, norm_mix_w[layer], w_in[layer], attn_lambda_qk[layer],
                         attn_subln_w[layer], mlstm_conv_w[layer], mlstm_conv_b[layer],
                         mlstm_i_b[layer], mlstm_f_b[layer], mlstm_norm_w[layer], w_out[layer],
                         norm_ffn_w[layer], peer_w_q[layer], peer_sub_keys[layer],
                         peer_u[layer], peer_v[layer], lambda_init)
    h = rmsnorm(h, norm_final_w)
    return h[:, N_META:]
```

```python
import numpy as np
from contextlib import ExitStack
import concourse.bass as bass
import concourse.mybir as mybir
from concourse.bass_utils import run_bass_kernel_spmd

F32 = mybir.dt.float32
BF16 = mybir.dt.bfloat16
I32 = mybir.dt.int32
U32 = mybir.dt.uint32
AF = mybir.ActivationFunctionType
ALU = mybir.AluOpType
AX = mybir.AxisListType

D = 2048
KC = 16
T = 2176
NB = 17
NSEQ = 2
NCORES = 8
EPS = 1e-6
C_AQ, C_AK, C_AV, C_MQ, C_MK, C_MV, C_MO, C_MI, C_MF = 0, 1024, 2048, 3072, 3584, 4096, 5120, 6144, 6148
INC = 6152
TG = [(0, 128), (128, 512), (640, 512), (1152, 512), (1664, 512)]
NEGBIG = -1.0e30
LAMBDA_INIT = 0.2


def L(method, *a, **k):
    return lambda e: getattr(e, method)(*a, **k)


class StopBuild(Exception):
    pass


import os
CUT = int(os.environ.get("KCUT", "0"))


def cut(n):
    if CUT == n:
        raise StopBuild()


class Buf:
    def __init__(self, name):
        self.name = name
        self.lw = None
        self.rd = {}
        self.dsem = None
        self.dcnt = 0


class Eng:
    def __init__(self, name, sem, is_pe=False):
        self.name = name
        self.sem = sem
        self.cnt = 0
        self.ops = []
        self.waited = {}
        self.is_pe = is_pe


class KB:
    def __init__(self, nc, es):
        self.nc = nc
        self.es = es
        self.pe = Eng("pe", es.enter_context(nc.semaphore("s_pe")), True)
        self.act = Eng("act", es.enter_context(nc.semaphore("s_act")))
        self.dve = Eng("dve", es.enter_context(nc.semaphore("s_dve")))
        self.pool = Eng("pool", es.enter_context(nc.semaphore("s_pool")))
        self.sp = Eng("sp", None)
        self.engs = [self.pe, self.act, self.dve, self.pool, self.sp]
        self.bufs = []
        self.bycname = {}
        self.nsem = 4

    def buf(self, name):
        if name in self.bycname:
            return self.bycname[name]
        b = Buf(name)
        self.bufs.append(b)
        self.bycname[name] = b
        return b

    def _need(self, eng, tok):
        sem, val, src = tok
        if src is eng and eng.is_pe:
            return
        key = id(sem)
        if eng.waited.get(key, 0) >= val:
            return
        eng.waited[key] = val
        eng.ops.append(L("wait_ge", sem, val))

    def _deps(self, eng, reads, writes):
        for b in reads:
            if b.lw is not None:
                self._need(eng, b.lw)
        for b in writes:
            if b.lw is not None:
                self._need(eng, b.lw)
            for t in b.rd.values():
                self._need(eng, t)

    def _commit(self, tok, reads, writes):
        for b in reads:
            b.rd[id(tok[0])] = tok
        for b in writes:
            b.lw = tok
            b.rd = {}

    def op(self, eng, fn, reads=(), writes=(), inc=True):
        ex = [b for b in reads if getattr(b, "excl", False) and b not in writes]
        if ex:
            writes = list(writes) + ex
        self._deps(eng, reads, writes)
        if inc:
            eng.cnt += 1
            tok = (eng.sem, eng.cnt, eng)
            eng.ops.append(lambda e, fn=fn, sem=eng.sem: fn(e).then_inc(sem, 1))
        else:
            tok = (eng.sem, eng.cnt + 1, eng)
            eng.ops.append(fn)
        self._commit(tok, reads, writes)

    def dma(self, q, fn, reads=(), writes=(), owner=None):
        assert owner is not None
        if owner.dsem is None:
            owner.dsem = self.es.enter_context(self.nc.semaphore("d_" + owner.name))
            self.nsem += 1
        self._deps(q, reads, writes)
        owner.dcnt += 1
        tok = (owner.dsem, 16 * owner.dcnt, None)
        q.ops.append(lambda e, fn=fn, sem=owner.dsem: fn(e).then_inc(sem, 16))
        self._commit(tok, reads, writes)

    def barrier(self):
        toks = []
        for e in (self.pe, self.act, self.dve, self.pool):
            if e.cnt:
                toks.append((e.sem, e.cnt, None))
        for b in self.bufs:
            if b.dsem is not None and b.dcnt:
                toks.append((b.dsem, 16 * b.dcnt, None))
        for e in self.engs:
            for t in toks:
                if e.sem is not None and t[0] is e.sem:
                    continue
                self._need(e, t)
        for b in self.bufs:
            b.lw = None
            b.rd = {}

    def emit(self):
        nc = self.nc
        with nc.Block() as block:
            @block.sync
            def _(e):
                for f in self.sp.ops:
                    f(e)

            @block.scalar
            def _(e):
                for f in self.act.ops:
                    f(e)

            @block.tensor
            def _(e):
                for f in self.pe.ops:
                    f(e)

            @block.vector
            def _(e):
                for f in self.dve.ops:
                    f(e)

            @block.gpsimd
            def _(e):
                for f in self.pool.ops:
                    f(e)


def build(stage=99, nseq=NSEQ, taps=None):
    nc = bass.Bass("TRN2", target_bir_lowering=False)
    es = ExitStack()
    dt_in = lambda name, shape, dt=F32: nc.dram_tensor(name, shape, dt, kind="ExternalInput").ap()
    xp = dt_in("xp", [NSEQ, T, D])
    nrm3 = dt_in("nrm3", [3, 128, D])
    w_in = dt_in("w_in", [D, INC])
    w_out = dt_in("w_out", [D, D])
    w_q = dt_in("w_q", [D, D])
    cosd = dt_in("cosd", [128, T])
    sind = dt_in("sind", [128, T])
    cmat = dt_in("cmat", [4, 128, 128])
    subln = dt_in("subln", [128, 128])
    mnw = dt_in("mnw", [128, 1024])
    convw = dt_in("convw", [128, 8, 4])
    convb = dt_in("convb", [128, 8])
    gbias = dt_in("gbias", [4, 2])
    lamqk = dt_in("lamqk", [1, 256])
    validc = dt_in("validc", [128, NB])
    skT = dt_in("skT", [128, 16, 128])
    full = stage >= 6
    if full:
        peer_u = dt_in("peer_u", [16384, D])
        peer_v = dt_in("peer_v", [16384, D])
    out = nc.dram_tensor("out", [NSEQ, 2048, D], F32, kind="ExternalOutput").ap()
    mixd = nc.dram_tensor("mixd", [NSEQ, 2048, D], BF16, kind="Internal").ap()
    xmidd = nc.dram_tensor("xmidd", [NSEQ, 2048, D], F32, kind="Internal").ap()
    gsc = nc.dram_tensor("gsc", [NSEQ, 2, 4, T], F32, kind="Internal").ap()
    if full:
        uvd = nc.dram_tensor("uvd", [16384, 2 * D], BF16, kind="Internal").ap()
    tapd = {}
    if taps:
        for name, shape in taps.items():
            tapd[name] = nc.dram_tensor("tap_" + name, shape, F32, kind="ExternalOutput").ap()

    AR = es.enter_context(nc.sbuf_tensor("arena", [128, 51200], F32))
    PS = [es.enter_context(nc.psum_tensor("ps%d" % i, [128, 512], F32)) for i in range(8)]
    kb = KB(nc, es)
    pe, act, dve, pool, sp = kb.pe, kb.act, kb.dve, kb.pool, kb.sp

    def carve(off, n, dt=F32, pat=None, parts=128, **kw):
        ap = AR[0:parts, off:off + n]
        if dt != F32:
            ap = ap.bitcast(dt)
        if pat:
            ap = ap.rearrange(pat, **kw)
        return ap

    o = 0

    def alloc(n):
        nonlocal o
        r = o
        o += n
        return r

    o_big = alloc(17408)
    o_qk = alloc(2176)
    o_va = alloc(2192)
    o_pre = alloc(2180)
    o_ycv = alloc(2176)
    o_sg = alloc(2048)
    o_crow = alloc(2176)
    o_nmrow = alloc(2176)
    o_e = alloc(512)
    o_dm = alloc(512)
    o_qbf = alloc(256)
    o_slab = alloc(4096)
    o_cos = alloc(2176)
    o_sin = alloc(2176)
    o_c = alloc(3000)
    o_tT = alloc(1024)
    o_misc = alloc(1024)
    assert o <= 51200, o

    oc = o_c
    identb = carve(oc, 64, BF16); oc += 64
    rmb = carve(oc, 64, BF16); oc += 64
    trib = carve(oc, 64, BF16); oc += 64
    identf = carve(oc, 128); oc += 128
    subln8 = carve(oc, 128); oc += 128
    mnwt = carve(oc, 1024); oc += 1024
    cwt = carve(oc, 32, F32, "p (c j) -> p c j", j=4); oc += 32
    cbt = carve(oc, 8); oc += 8
    gbt = carve(oc, 2, parts=4); oc += 2
    nfb = carve(oc, 1, parts=4); oc += 1
    vct = carve(oc, NB); oc += NB
    neglam = carve(oc, 1); oc += 1
    lamrow = carve(oc, 256, parts=1); oc += 256
    lamtmp = carve(oc, 8, parts=1); oc += 8
    onesrow = carve(oc, 512, parts=1); oc += 512
    ones4 = None
    cstage = carve(oc, 128); oc += 128
    iota16 = carve(oc, 16); oc += 16
    lo16 = carve(oc, 16); oc += 16
    assert oc <= o_c + 3000

    B = kb.buf
    b_const = B("const")

    xnT = carve(o_big, 17408, BF16, "p (k t) -> p k t", k=KC)
    b_xnT = B("xnT")
    qT = carve(o_qk, 1088, BF16)
    kT = carve(o_qk + 1088, 1088, BF16)
    b_qT, b_kT = B("qT"), B("kT")
    vaug = carve(o_va, 2192, BF16)[:, 0:NB * 257].rearrange("p (b e) -> p b e", e=257)
    b_vaug = [B("vaug%d" % i) for i in range(NB)]
    pre = carve(o_pre, 2180)
    ycv = carve(o_ycv, 2176)
    b_pre, b_ycv = B("pre"), B("ycv")
    crow = carve(o_crow, 2176, parts=1)
    nmrow = carve(o_nmrow, 2176, parts=1)
    b_crow, b_nmrow = B("crow"), B("nmrow")
    ebuf = [carve(o_e + 256 * i, 256, BF16) for i in range(2)]
    b_e = [B("e0"), B("e1")]
    dmb = carve(o_dm, 512)
    b_dm = B("dm")
    qbf = carve(o_qbf, 256, BF16)
    b_qbf = B("qbf")
    slab = [carve(o_slab + 1024 * i, 1024, BF16, "p (k c) -> p k c", k=KC) for i in range(4)]
    b_slab = [B("slab%d" % i) for i in range(4)]
    cost = carve(o_cos, 2176)
    sint = carve(o_sin, 2176)
    tT = carve(o_tT, 1024, BF16, "p (k t) -> p k t", k=KC)
    b_tT = B("tT")
    b_ps = [B("ps%d" % i) for i in range(8)]
    for _b in b_ps:
        _b.excl = True
    psb = [PS[i].ap().bitcast(BF16) for i in range(8)]
    psf = [PS[i].ap() for i in range(8)]

    om = o_misc
    ss = carve(om, 8); om += 8
    b_ss = B("ss")
    small = carve(om, 64); om += 64
    b_small = B("small")
    emtT = carve(om, 64); om += 64
    b_emtT = B("emtT")
    wg = carve(om, 64, BF16, "p (k c) -> p k c", k=KC); om += 64
    b_wg = B("wg")
    mixst = carve(om, 128, BF16); om += 128
    b_mixst = B("mixst")

    o_p1 = o_qk
    xt = [carve(o_p1, 2048), carve(o_p1 + 2048, 2048)]
    xb = carve(o_p1 + 4096, 1024, BF16)
    nrm = carve(o_p1 + 5120, 2048)
    nrm2 = carve(o_p1 + 7168, 2048)
    b_xt = [B("xt0"), B("xt1")]
    b_xb = B("xb")
    b_nrm = B("nrm")
    b_nrm2 = B("nrm2")

    def tap(name, src_ap, bufs, parts=128):
        if name in tapd:
            bt = B("tap_" + name)
            kb.barrier()
            kb.dma(pool, L("dma_start", out=tapd[name], in_=src_ap, max_dma_last_dim=2048), reads=bufs,
                   writes=[bt], owner=bt)
            kb.tapbufs.append(bt)
    kb.tapbufs = []

    def ld(dst, src, b=b_const, q=sp):
        kb.dma(q, L("dma_start", out=dst, in_=src), writes=[b], owner=b)

    ld(mnwt, mnw)
    ld(cwt, convw)
    ld(cbt, convb)
    ld(gbt, gbias)
    ld(vct, validc)
    ld(lamrow, lamqk)
    ld(subln8, subln)
    ld(identf, cmat[0])
    ld(iota16, cmat[3][:, 0:16])
    ld(lo16, cmat[3][:, 16:32])
    b_cst = B("cstage")
    for i, dstb in enumerate([identb, rmb, trib]):
        kb.dma(sp, L("dma_start", out=cstage, in_=cmat[i]), writes=[b_cst], owner=b_cst)
        kb.op(dve, L("tensor_copy", out=dstb, in_=cstage), reads=[b_cst], writes=[b_const])
    kb.op(dve, L("tensor_scalar", out=subln8, in0=subln8, scalar1=1.0 - LAMBDA_INIT, scalar2=None, op0=ALU.mult),
          reads=[b_const], writes=[b_const])
    kb.op(dve, L("tensor_scalar", out=nfb, in0=gbt[:, 1:2], scalar1=-1.0, scalar2=None, op0=ALU.mult),
          reads=[b_const], writes=[b_const])
    kb.op(dve, L("memset", onesrow, 1.0), writes=[b_const])
    kb.op(dve, L("memset", lamtmp, 0.0), writes=[b_const])
    kb.op(dve, L("scalar_tensor_tensor", out=lamrow[:, 0:64], in0=lamrow[:, 0:64], scalar=1.0, in1=lamrow[:, 64:128],
                                                op0=ALU.mult, op1=ALU.mult, accum_out=lamtmp[:, 0:1]),
          reads=[b_const], writes=[b_const])
    kb.op(dve, L("scalar_tensor_tensor", out=lamrow[:, 128:192], in0=lamrow[:, 128:192], scalar=1.0,
                                                in1=lamrow[:, 192:256], op0=ALU.mult, op1=ALU.mult,
                                                accum_out=lamtmp[:, 1:2]),
          reads=[b_const], writes=[b_const])
    kb.op(act, L("activation", out=lamtmp[:, 2:4], in_=lamtmp[:, 0:2], func=AF.Exp), reads=[b_const], writes=[b_const])
    kb.op(dve, L("tensor_tensor", out=lamtmp[:, 4:5], in0=lamtmp[:, 3:4], in1=lamtmp[:, 2:3], op=ALU.subtract),
          reads=[b_const], writes=[b_const])
    kb.op(dve, L("tensor_scalar", out=lamtmp[:, 5:6], in0=lamtmp[:, 4:5], scalar1=-LAMBDA_INIT, scalar2=None,
                                         op0=ALU.add), reads=[b_const], writes=[b_const])
    kb.op(pe, L("matmul", psf[0][:, 0:8], lhsT=onesrow[:, 0:128], rhs=lamtmp[:, 0:8], start=True, stop=True),
          reads=[b_const], writes=[b_ps[0]])
    kb.op(act, L("copy", out=neglam, in_=psf[0][:, 5:6]), reads=[b_ps[0]], writes=[b_const])

    def rmsnorm_to(xin, b_xin, nrmt, b_nrmt, dst, b_dst, d=D):
        kb.op(act, L("activation", out=dst, in_=xin, func=AF.Square, accum_out=ss[:, 0:1]),
              reads=[b_xin], writes=[b_dst, b_ss])
        kb.op(act, L("activation", out=ss[:, 1:2], in_=ss[:, 0:1], func=AF.Sqrt, scale=1.0 / d, bias=EPS),
              reads=[b_ss], writes=[b_ss])
        kb.op(dve, L("reciprocal", out=ss[:, 2:3], in_=ss[:, 1:2]), reads=[b_ss], writes=[b_ss])
        kb.op(dve, L("scalar_tensor_tensor", out=dst, in0=xin, scalar=ss[:, 2:3], in1=nrmt, op0=ALU.mult,
                                                    op1=ALU.mult), reads=[b_xin, b_ss, b_nrmt], writes=[b_dst])

    def transpose16(src, b_src, dstfn, b_dst, bank0=0):
        for g in range(4):
            bk = bank0 + (g % 2)
            for j in range(4):
                kc = g * 4 + j
                kb.op(pe, L("transpose", out=psb[bk][:, j * 128:(j + 1) * 128],
                                                                    in_=src[:, kc * 128:(kc + 1) * 128], identity=identb),
                      reads=[b_src, b_const], writes=[b_ps[bk]], inc=(j == 3))
            kb.op(act, L("copy", out=dstfn(g), in_=psb[bk][:, 0:512].rearrange("p (k t) -> p k t", k=4)),
                  reads=[b_ps[bk]], writes=[b_dst])

    def load_slab(slot, col0, ncols=128, src=None):
        src = w_in if src is None else src
        kb.dma(pool, L("dma_start",
            out=slab[slot][:, :, 0:ncols], in_=src[:, col0:col0 + ncols].rearrange("(k p) c -> p k c", p=128)),
            writes=[b_slab[slot]], owner=b_slab[slot])

    def proj_fm(slot, ncols, tok0, ntok, bank, lhs=None, b_lhs=None):
        for kc in range(KC):
            l = slab[slot][:, kc, 0:ncols] if lhs is None else lhs(kc)
            kb.op(pe, L("matmul", psf[bank][0:ncols, 0:ntok], lhsT=l, rhs=xnT[:, kc, tok0:tok0 + ntok],
                                                     start=(kc == 0), stop=(kc == KC - 1)),
                  reads=[b_slab[slot] if b_lhs is None else b_lhs, b_xnT], writes=[b_ps[bank]], inc=(kc == KC - 1))

    slab2 = [carve(o_slab + 2048 * i, 2048, BF16, "p (k c) -> p k c", k=KC) for i in range(2)]

    def load_slab256(pair, col0):
        kb.dma(pool, L("dma_start", out=slab2[pair], in_=w_in[:, col0:col0 + 256].rearrange("(k p) c -> p k c", p=128)),
               writes=[b_slab[2 * pair], b_slab[2 * pair + 1]], owner=b_slab[2 * pair])

    def proj_tm256(pair, blk, bank):
        for kc in range(KC):
            kb.op(pe, L("matmul", psf[bank][:, 0:256], lhsT=xnT[:, kc, blk * 128:(blk + 1) * 128], rhs=slab2[pair][:, kc, :],
                        start=(kc == 0), stop=(kc == KC - 1)),
                  reads=[b_slab[2 * pair], b_slab[2 * pair + 1], b_xnT], writes=[b_ps[bank]], inc=(kc == KC - 1))

    def proj_tm(slots, blk, bank):
        for hi, slot in enumerate(slots):
            for kc in range(KC):
                kb.op(pe, L("matmul",
                    psf[bank][:, hi * 128:(hi + 1) * 128], lhsT=xnT[:, kc, blk * 128:(blk + 1) * 128],
                    rhs=slab[slot][:, kc, :], start=(kc == 0), stop=(kc == KC - 1)),
                    reads=[b_slab[slot], b_xnT], writes=[b_ps[bank]], inc=(kc == KC - 1 and hi == len(slots) - 1))

    if full:
        cf = [carve(o_big + 2048 * i, 2048) for i in range(4)]
        cbf = [carve(o_big + 8192 + 1024 * i, 1024, BF16) for i in range(4)]
        b_cf = [B("cf%d" % i) for i in range(4)]
        b_cbf = [B("cbf%d" % i) for i in range(4)]
        n = 0
        for (src, dstd) in ((peer_u, uvd[:, 0:D]), (peer_v, uvd[:, D:2 * D])):
            for i in range(128):
                q = n % 4
                kb.dma(sp, L("dma_start", out=cf[q], in_=src[i * 128:(i + 1) * 128, :]), writes=[b_cf[q]], owner=b_cf[q])
                if n % 2:
                    kb.op(act, L("copy", out=cbf[q], in_=cf[q]), reads=[b_cf[q]], writes=[b_cbf[q]])
                else:
                    kb.op(dve, L("tensor_copy", out=cbf[q], in_=cf[q]), reads=[b_cf[q]], writes=[b_cbf[q]])
                kb.dma(pool, L("dma_start", out=dstd[i * 128:(i + 1) * 128, :], in_=cbf[q]), reads=[b_cbf[q]], writes=[],
                       owner=b_cbf[q])
                n += 1

    for s in range(nseq):
      try:
          kb.barrier()
          kb.dma(sp, L("dma_start", out=nrm, in_=nrm3[0]), writes=[b_nrm], owner=b_nrm)
          for b in range(NB):
              x = xt[b % 2]
              bx = b_xt[b % 2]
              kb.dma(sp, L("dma_start", out=x, in_=xp[s, b * 128:(b + 1) * 128, :]), writes=[bx], owner=bx)
              rmsnorm_to(x, bx, nrm, b_nrm, xb, b_xb)
              transpose16(xb, b_xb, lambda g, b=b: xnT[:, g * 4:(g + 1) * 4, b * 128:(b + 1) * 128], b_xnT)
          if s == 0:
              tap("xnT", xnT[:, 0, :], [b_xnT])
          if stage < 2:
              continue
          kb.barrier()

          b_cs = B("cossin")
          kb.dma(sp, L("dma_start", out=cost, in_=cosd), writes=[b_cs], owner=b_cs)
          b_cs2 = B("cossin2")
          kb.dma(sp, L("dma_start", out=sint, in_=sind), writes=[b_cs2], owner=b_cs2)
          kb.barrier()
          for h in range(8):
              load_slab(0, C_AQ + h * 128)
              load_slab(1, C_AK + h * 128)
              load_slab(2, C_AV + h * 128)
              if h == 0: cut(1)
              for (tok0, ntok) in TG:
                  for which in range(2):
                      dst, b_dst = (qT, b_qT) if which == 0 else (kT, b_kT)
                      proj_fm(which, 128, tok0, ntok, 0)
                      cut(6)
                      kb.op(act, L("copy", out=qbf[:, 0:ntok], in_=psf[0][:, 0:ntok]),
                            reads=[b_ps[0]], writes=[b_qbf])
                      cut(7)
                      kb.op(dve, L("tensor_tensor",
                          out=pre[:, 0:ntok], in0=psf[0][:, 0:ntok], in1=cost[:, tok0:tok0 + ntok], op=ALU.mult),
                          reads=[b_ps[0], b_const], writes=[b_pre])
                      cut(4)
                      kb.op(pe, L("matmul", psf[1][:, 0:ntok], lhsT=rmb, rhs=qbf[:, 0:ntok], start=True,
                                                              stop=True), reads=[b_qbf, b_const], writes=[b_ps[1]])
                      kb.op(dve, L("tensor_tensor",
                          out=ycv[:, 0:ntok], in0=psf[1][:, 0:ntok], in1=sint[:, tok0:tok0 + ntok], op=ALU.mult),
                          reads=[b_ps[1], b_const], writes=[b_ycv])
                      kb.op(dve, L("tensor_tensor",
                          out=dst[:, tok0:tok0 + ntok], in0=pre[:, 0:ntok], in1=ycv[:, 0:ntok], op=ALU.add),
                          reads=[b_pre, b_ycv], writes=[b_dst])
              cut(5)
              for b in range(NB):
                  proj_tm([2], b, 2 + (b % 2))
                  kb.op(act, L("copy", out=vaug[:, b, 0:128], in_=psf[2 + (b % 2)][:, 0:128]),
                        reads=[b_ps[2 + (b % 2)]], writes=[b_vaug[b]])
                  kb.op(dve, L("tensor_copy", out=vaug[:, b, 128:129], in_=vct[:, b:b + 1]),
                        reads=[b_const], writes=[b_vaug[b]])
              if s == 0 and h == 0:
                  tap("qT0", qT, [b_qT])
                  tap("kT0", kT, [b_kT])
              if h == 0: cut(3)
              a1buf = carve(o_sg, 512)
              att = carve(o_sg + 512, 128)
              junk = carve(o_sg + 640, 128)
              for g in range(4):
                  first = 1 + 4 * g
                  tq0 = 128 * first
                  ei = 0
                  for c in range(2):
                      for j in range(first + 4):
                          r = j - first
                          c0 = max(r, 0) * 128
                          bk = 0 + (ei % 2)
                          eb = ebuf[ei % 2]
                          b_eb = b_e[ei % 2]
                          ei += 1
                          lk = kT[c * 64:(c + 1) * 64, j * 128:(j + 1) * 128]
                          if r >= 0:
                              kb.op(pe, L("matmul",
                                  psf[bk][:, c0:c0 + 128], lhsT=lk, rhs=qT[c * 64:(c + 1) * 64, tq0 + c0:tq0 + c0 + 128],
                                  start=True, stop=False), reads=[b_kT, b_qT], writes=[b_ps[bk]])
                              kb.op(pe, L("matmul", psf[bk][:, c0:c0 + 128], lhsT=identb, rhs=trib,
                                                                         start=False, stop=True),
                                    reads=[b_const], writes=[b_ps[bk]])
                              c1 = c0 + 128
                          else:
                              c1 = c0
                          if c1 < 512:
                              kb.op(pe, L("matmul",
                                  psf[bk][:, c1:512], lhsT=lk, rhs=qT[c * 64:(c + 1) * 64, tq0 + c1:tq0 + 512],
                                  start=True, stop=True), reads=[b_kT, b_qT], writes=[b_ps[bk]])
                          kb.op(act, L("activation", out=eb[:, c0:512], in_=psf[bk][:, c0:512],
                                                                                 func=AF.Exp, scale=0.125),
                                reads=[b_ps[bk]], writes=[b_eb])
                          for il in range(max(r, 0), 4):
                              i = first + il
                              ob = 4 + il
                              kb.op(pe, L("matmul",
                                  psf[ob][:, 0:129], lhsT=eb[:, il * 128:(il + 1) * 128],
                                  rhs=vaug[:, j, 0:129], start=(j == 0), stop=(j == i)),
                                  reads=[b_eb, b_vaug[j]], writes=[b_ps[ob]])
                      for il in range(4):
                          i = first + il
                          ob = 4 + il
                          Oc = psf[ob][:, 0:128]
                          a1 = a1buf[:, il * 128:(il + 1) * 128]
                          kb.op(dve, L("reciprocal", out=small[:, c:c + 1], in_=psf[ob][:, 128:129]),
                                reads=[b_ps[ob]], writes=[b_small])
                          if c == 0:
                              kb.op(act, L("activation", out=a1, in_=Oc, func=AF.Copy, scale=small[:, 0:1]),
                                    reads=[b_ps[ob], b_small], writes=[b_dm])
                              continue
                          kb.op(dve, L("tensor_tensor", out=small[:, 2:3], in0=small[:, 1:2], in1=neglam, op=ALU.mult),
                                reads=[b_small, b_const], writes=[b_small])
                          kb.op(dve, L("scalar_tensor_tensor", out=att, in0=Oc, scalar=small[:, 2:3],
                                                                                   in1=a1, op0=ALU.mult, op1=ALU.add),
                                reads=[b_ps[ob], b_small, b_dm], writes=[b_dm])
                          kb.op(act, L("activation", out=junk, in_=att, func=AF.Square, accum_out=small[:, 3:4]),
                                reads=[b_dm], writes=[b_dm, b_small])
                          kb.op(act, L("activation", out=small[:, 4:5], in_=small[:, 3:4], func=AF.Sqrt,
                                                            scale=1.0 / 128, bias=EPS), reads=[b_small], writes=[b_small])
                          kb.op(dve, L("reciprocal", out=small[:, 5:6], in_=small[:, 4:5]), reads=[b_small],
                                writes=[b_small])
                          kb.op(dve, L("scalar_tensor_tensor", out=mixst[:, 0:128], in0=att, scalar=small[:, 5:6],
                                                                      in1=subln8, op0=ALU.mult, op1=ALU.mult),
                                reads=[b_dm, b_small, b_const], writes=[b_mixst])
                          kb.dma(sp, L("dma_start",
                              out=mixd[s, (i - 1) * 128:i * 128, h * 128:(h + 1) * 128], in_=mixst[:, 0:128]),
                              reads=[b_mixst], writes=[], owner=b_mixst)
          if s == 0 and stage == 2:
              tap("mix", mixd[0][:, 0:1024], [])
          if stage < 3:
              continue
          kb.barrier()
          LI = carve(o_qk, 2176, parts=4)
          EX = carve(o_va, 2176, parts=4)
          Bt = carve(o_pre, 2176, parts=4)
          Ct = carve(o_ycv, 2176, parts=4)
          ONES4 = carve(o_crow, 2176, parts=4)
          b_g = B("gates")
          kb.dma(pool, L("dma_start", out=wg, in_=w_in[:, C_MI:C_MI + 8].rearrange("(k p) c -> p k c", p=128)),
                 writes=[b_wg], owner=b_wg)
          for (tok0, ntok) in TG:
              proj_fm(None, 4, tok0, ntok, 0, lhs=lambda kc: wg[:, kc, 0:4], b_lhs=b_wg)
              kb.op(act, L("activation", out=LI[:, tok0:tok0 + ntok], in_=psf[0][0:4, 0:ntok], func=AF.Identity,
                           bias=gbt[:, 0:1]), reads=[b_ps[0], b_const], writes=[b_g])
              proj_fm(None, 4, tok0, ntok, 1, lhs=lambda kc: wg[:, kc, 4:8], b_lhs=b_wg)
              kb.op(act, L("activation", out=EX[:, tok0:tok0 + ntok], in_=psf[1][0:4, 0:ntok], func=AF.Exp, scale=-1.0,
                           bias=nfb[:, 0:1]), reads=[b_ps[1], b_const], writes=[b_g])
          G = [b_g]
          kb.op(act, L("activation", out=EX, in_=EX, func=AF.Ln, bias=1.0), reads=G, writes=G)
          kb.op(dve, L("tensor_scalar", out=EX, in0=EX, scalar1=-1.0, scalar2=None, op0=ALU.mult), reads=G, writes=G)
          kb.op(dve, L("memset", EX[:, 0:112], 0.0), writes=G)
          kb.op(dve, L("memset", LI[:, 0:112], NEGBIG), writes=G)
          kb.op(dve, L("memset", ONES4, 1.0), writes=G)
          kb.op(dve, L("tensor_tensor_scan", out=Bt, data0=ONES4, data1=EX, initial=0.0, op0=ALU.mult, op1=ALU.add),
                reads=G, writes=G)
          kb.op(dve, L("tensor_tensor", out=Ct, in0=LI, in1=Bt, op=ALU.subtract), reads=G, writes=G)
          Mt = LI
          kb.op(dve, L("tensor_tensor_scan", out=Mt, data0=Ct, data1=Ct, initial=-3.0e38, op0=ALU.max, op1=ALU.max),
                reads=G, writes=G)
          NEGM = EX
          kb.op(dve, L("tensor_scalar", out=NEGM, in0=Mt, scalar1=-1.0, scalar2=None, op0=ALU.mult), reads=G, writes=G)
          kb.op(dve, L("tensor_tensor", out=Bt, in0=Bt, in1=Mt, op=ALU.add), reads=G, writes=G)
          kb.op(dve, L("tensor_scalar", out=Bt, in0=Bt, scalar1=-80.0, scalar2=None, op0=ALU.max), reads=G, writes=G)
          kb.op(act, L("activation", out=Bt, in_=Bt, func=AF.Exp, scale=-1.0), reads=G, writes=G)
          for i in range(1, NB):
              kb.op(pe, L("matmul", psf[2][:, (i - 1) * 4:i * 4], lhsT=Bt[:, i * 128:(i + 1) * 128], rhs=identf[0:4, 0:4],
                          start=True, stop=True), reads=G + [b_const], writes=[b_ps[2]])
          kb.op(act, L("copy", out=emtT, in_=psf[2][:, 0:64]), reads=[b_ps[2]], writes=[b_emtT])
          b_gsc = B("gsc")
          kb.dma(sp, L("dma_start", out=gsc[s, 0], in_=Ct), reads=G, writes=[b_gsc], owner=b_g)
          kb.dma(sp, L("dma_start", out=gsc[s, 1], in_=NEGM), reads=G, writes=[b_gsc], owner=b_g)
          if s == 0:
              tap("emtT", emtT, [b_emtT])
          kb.barrier()
          kb.op(dve, L("memset", pre[:, 0:3], 0.0), writes=[b_pre])
          for b in range(NB):
              kb.op(dve, L("memset", vaug[:, b, 256:257], 1.0), writes=[b_vaug[b]])
          hmbuf = carve(o_sg, 256)
          hjunk = carve(o_sg + 256, 256)
          sgbuf = carve(o_sg + 512, 256)
          b_hm = B("hm")
          for h in range(4):
              load_slab(0, C_MQ + h * 128)
              load_slab(1, C_MK + h * 128)
              kb.dma(sp, L("dma_start", out=crow, in_=gsc[s, 0, h:h + 1, :]), reads=[b_gsc], writes=[b_crow], owner=b_crow)
              kb.dma(sp, L("dma_start", out=nmrow, in_=gsc[s, 1, h:h + 1, :]), reads=[b_gsc], writes=[b_nmrow],
                     owner=b_nmrow)
              for which in range(2):
                  dst, b_dst = (qT, b_qT) if which == 0 else (kT, b_kT)
                  ch = which * 4 + h
                  for (tok0, ntok) in TG:
                      proj_fm(which, 128, tok0, ntok, which)
                      kb.op(act, L("copy", out=pre[:, 3 + tok0:3 + tok0 + ntok], in_=psf[which][:, 0:ntok]),
                            reads=[b_ps[which]], writes=[b_pre])
                  kb.op(dve, L("tensor_scalar", out=ycv, in0=pre[:, 0:T], scalar1=cwt[:, ch, 0:1], scalar2=cbt[:, ch:ch + 1],
                               op0=ALU.mult, op1=ALU.add), reads=[b_pre, b_const], writes=[b_ycv])
                  for j in range(1, 4):
                      kb.op(dve, L("scalar_tensor_tensor", out=ycv, in0=pre[:, j:T + j], scalar=cwt[:, ch, j:j + 1], in1=ycv,
                                   op0=ALU.mult, op1=ALU.add), reads=[b_pre, b_ycv, b_const], writes=[b_ycv])
                  kb.op(act, L("activation", out=dst, in_=ycv, func=AF.Silu), reads=[b_ycv], writes=[b_dst])
              load_slab256(1, C_MV + h * 256)
              for b in range(NB):
                  proj_tm256(1, b, 2 + (b % 2))
                  kb.op(act, L("copy", out=vaug[:, b, 0:256], in_=psf[2 + (b % 2)][:, 0:256]), reads=[b_ps[2 + (b % 2)]],
                        writes=[b_vaug[b]])
              load_slab256(0, C_MO + h * 256)
              if s == 0 and h == 0:
                  tap("mq0", qT, [b_qT])
                  tap("mk0", kT, [b_kT])
              for g in range(4):
                  first = 1 + 4 * g
                  tq0 = 128 * first
                  for j in range(first + 4):
                      r = j - first
                      c0 = max(r, 0) * 128
                      ab = ebuf[j % 2]
                      b_ab = b_e[j % 2]
                      jb = slice(j * 128, (j + 1) * 128)
                      kb.op(pe, L("matmul", psf[0][:, c0:512], lhsT=kT[:, jb], rhs=qT[:, tq0 + c0:tq0 + 512], start=True,
                                  stop=True), reads=[b_kT, b_qT], writes=[b_ps[0]])
                      if r >= 0:
                          kb.op(pe, L("matmul", psf[1][:, c0:c0 + 128], lhsT=crow[:, jb], rhs=onesrow[:, 0:128], start=True,
                                      stop=False), reads=[b_crow, b_const], writes=[b_ps[1]])
                          kb.op(pe, L("matmul", psf[1][:, c0:c0 + 128], lhsT=onesrow[:, 0:128],
                                      rhs=nmrow[:, tq0 + c0:tq0 + c0 + 128], start=False, stop=False),
                                reads=[b_nmrow, b_const], writes=[b_ps[1]])
                          kb.op(pe, L("matmul", psf[1][:, c0:c0 + 128], lhsT=identb, rhs=trib, start=False, stop=True),
                                reads=[b_const], writes=[b_ps[1]])
                          c1 = c0 + 128
                      else:
                          c1 = c0
                      if c1 < 512:
                          kb.op(pe, L("matmul", psf[1][:, c1:512], lhsT=crow[:, jb], rhs=onesrow[:, 0:512 - c1], start=True,
                                      stop=False), reads=[b_crow, b_const], writes=[b_ps[1]])
                          kb.op(pe, L("matmul", psf[1][:, c1:512], lhsT=onesrow[:, 0:128], rhs=nmrow[:, tq0 + c1:tq0 + 512],
                                      start=False, stop=True), reads=[b_nmrow, b_const], writes=[b_ps[1]])
                      kb.op(act, L("activation", out=dmb[:, c0:512], in_=psf[1][:, c0:512], func=AF.Exp),
                            reads=[b_ps[1]], writes=[b_dm])
                      kb.op(dve, L("scalar_tensor_tensor", out=ab[:, c0:512], in0=psf[0][:, c0:512], scalar=128.0 ** -0.5,
                                   in1=dmb[:, c0:512], op0=ALU.mult, op1=ALU.mult), reads=[b_ps[0], b_dm], writes=[b_ab])
                      for il in range(max(r, 0), 4):
                          kb.op(pe, L("matmul", psf[4 + il][:, 0:257], lhsT=ab[:, il * 128:(il + 1) * 128],
                                      rhs=vaug[:, j, 0:257], start=(j == 0), stop=(j == first + il)),
                                reads=[b_ab, b_vaug[j]], writes=[b_ps[4 + il]])
                  for il in range(4):
                      i = first + il
                      ob = 4 + il
                      e0 = (i - 1) * 4 + h
                      kb.op(act, L("activation", out=small[:, 0:1], in_=psf[ob][:, 256:257], func=AF.Abs),
                            reads=[b_ps[ob]], writes=[b_small])
                      kb.op(dve, L("tensor_tensor", out=small[:, 1:2], in0=small[:, 0:1], in1=emtT[:, e0:e0 + 1], op=ALU.max),
                            reads=[b_small, b_emtT], writes=[b_small])
                      kb.op(dve, L("reciprocal", out=small[:, 2:3], in_=small[:, 1:2]), reads=[b_small], writes=[b_small])
                      kb.op(act, L("activation", out=hmbuf, in_=psf[ob][:, 0:256], func=AF.Copy, scale=small[:, 2:3]),
                            reads=[b_ps[ob], b_small], writes=[b_hm])
                      kb.op(act, L("activation", out=hjunk, in_=hmbuf, func=AF.Square, accum_out=small[:, 3:4]),
                            reads=[b_hm], writes=[b_hm, b_small])
                      kb.op(act, L("activation", out=small[:, 4:5], in_=small[:, 3:4], func=AF.Sqrt, scale=1.0 / 256,
                                   bias=EPS), reads=[b_small], writes=[b_small])
                      kb.op(dve, L("reciprocal", out=small[:, 5:6], in_=small[:, 4:5]), reads=[b_small], writes=[b_small])
                      proj_tm256(0, i, 2)
                      kb.op(act, L("activation", out=sgbuf, in_=psf[2][:, 0:256], func=AF.Sigmoid), reads=[b_ps[2]],
                            writes=[b_hm])
                      kb.op(dve, L("scalar_tensor_tensor", out=hjunk, in0=hmbuf, scalar=small[:, 5:6],
                                   in1=mnwt[:, h * 256:(h + 1) * 256], op0=ALU.mult, op1=ALU.mult),
                            reads=[b_hm, b_small, b_const], writes=[b_hm])
                      kb.op(dve, L("tensor_tensor", out=mixst[:, 0:256], in0=hjunk, in1=sgbuf, op=ALU.mult), reads=[b_hm],
                            writes=[b_mixst])
                      kb.dma(sp, L("dma_start", out=mixd[s, (i - 1) * 128:i * 128, 1024 + h * 256:1024 + (h + 1) * 256],
                                   in_=mixst[:, 0:256]), reads=[b_mixst], writes=[], owner=b_mixst)
          if s == 0:
              tap("mix", mixd[0], [])
          if stage < 4:
              continue
          kb.barrier()
          wbig = carve(o_big, 16384, BF16, "p (k c) -> p k c", k=KC)
          b_wb = [B("wbig%d" % n) for n in range(4)]

          def load_wbig(src):
              for n in range(4):
                  kb.dma(pool, L("dma_start", out=wbig[:, :, n * 512:(n + 1) * 512],
                                 in_=src[:, n * 512:(n + 1) * 512].rearrange("(k p) c -> p k c", p=128)),
                         writes=[b_wb[n]], owner=b_wb[n])
          load_wbig(w_out)
          for blk in range(1, NB):
              r0 = (blk - 1) * 128
              kb.dma(sp, L("dma_start", out=xb, in_=mixd[s, r0:r0 + 128, :]), writes=[b_xb], owner=b_xb)
              kb.dma(sp, L("dma_start", out=xt[0], in_=xp[s, blk * 128:(blk + 1) * 128, :]), writes=[b_xt[0]], owner=b_xt[0])
              transpose16(xb, b_xb, lambda g: tT[:, g * 4:(g + 1) * 4, :], b_tT, bank0=0)
              for n in range(4):
                  for kc in range(KC):
                      kb.op(pe, L("matmul", psf[4 + n][:, 0:512], lhsT=tT[:, kc, :], rhs=wbig[:, kc, n * 512:(n + 1) * 512],
                                  start=(kc == 0), stop=(kc == KC - 1)), reads=[b_tT, b_wb[n]], writes=[b_ps[4 + n]],
                        inc=(kc == KC - 1))
                  kb.op(dve, L("tensor_tensor", out=xt[1][:, n * 512:(n + 1) * 512], in0=psf[4 + n][:, 0:512],
                               in1=xt[0][:, n * 512:(n + 1) * 512], op=ALU.add), reads=[b_ps[4 + n], b_xt[0]],
                        writes=[b_xt[1]])
              kb.dma(sp, L("dma_start", out=xmidd[s, r0:r0 + 128, :], in_=xt[1]), reads=[b_xt[1]], writes=[], owner=b_xt[1])
          if s == 0:
              tap("xmid", xmidd[0], [])
          if stage < 5:
              continue
          o_r1 = o_crow
          skb = carve(o_big + 16384, 1024, BF16, "p (c n) -> p c n", c=16)
          qTc = carve(o_r1, 1024, BF16, "p (c t) -> p c t", c=16)
          sc = carve(o_r1 + 1024, 2048, F32, "p (c n) -> p c n", c=16)
          sc2 = carve(o_r1 + 3072, 2048, F32, "p (c n) -> p c n", c=16)
          cand_s = carve(o_r1 + 5120, 2048, F32, "p (h n) -> p h n", h=8)
          cand_i = carve(o_r1 + 1024, 2048, F32, "p (h n) -> p h n", h=8)
          cand2 = carve(o_r1 + 3072, 2048, F32, "p (h n) -> p h n", h=8)
          osm = o_r1 + 7168
          top = carve(osm, 256, F32, "p (c k) -> p c k", c=16); osm += 256
          ti = carve(osm, 256, U32, "p (c k) -> p c k", c=16); osm += 256
          tif = carve(osm, 256, F32, "p (c k) -> p c k", c=16); osm += 256
          junkc = carve(osm, 256); osm += 256
          best = carve(osm, 128, F32, "p (h k) -> p h k", h=8); osm += 128
          idxf = carve(osm, 128); osm += 128
          exb = carve(osm, 128, F32, "p (h k) -> p h k", h=8); osm += 128
          smb = carve(osm, 16); osm += 16
          posu = carve(osm, 128, U32, "p (h k) -> p h k", h=8); osm += 128
          posf = carve(osm, 128, F32, "p (h k) -> p h k", h=8); osm += 128
          paf = carve(osm, 128, F32, "p (h k) -> p h k", h=8); osm += 128
          pbf = carve(osm, 128, F32, "p (h k) -> p h k", h=8); osm += 128
          i2sel = carve(osm, 128); osm += 128
          assert osm <= o_r1 + 9728
          eqb = carve(o_r1 + 1024, 2048)
          b_pos = B("p4pos")
          idx_all = carve(o_sin, 2048, I32, "p (b k) -> p b k", b=16)
          gate_all = carve(o_cos, 2048, F32, "p (b k) -> p b k", b=16)
          o_r3 = o_p1 + 9216
          actr = carve(o_r3, 128)
          gab = carve(o_r3 + 128, 128)
          coef = carve(o_r3 + 256, 128)
          b_sc, b_sc2, b_cs_, b_top, b_ti, b_best, b_idxf, b_ex = [B("p4_%d" % q) for q in range(8)]
          b_qTc, b_skb, b_idx, b_gate, b_y, b_actr = [B("p4b_%d" % q) for q in range(6)]
          for half in range(1):
              kb.barrier()
              load_wbig(w_q)
              kb.dma(pool, L("dma_start", out=skb, in_=skT), writes=[b_skb], owner=b_skb)
              kb.dma(sp, L("dma_start", out=nrm, in_=nrm3[1]), writes=[b_nrm], owner=b_nrm)
              kb.dma(sp, L("dma_start", out=nrm2, in_=nrm3[2]), writes=[b_nrm2], owner=b_nrm2)
              for bl in range(16):
                  blk = 1 + bl
                  r0 = (blk - 1) * 128
                  x = xt[bl % 2]
                  bx = b_xt[bl % 2]
                  kb.dma(sp, L("dma_start", out=x, in_=xmidd[s, r0:r0 + 128, :]), writes=[bx], owner=bx)
                  rmsnorm_to(x, bx, nrm, b_nrm, xb, b_xb)
                  transpose16(xb, b_xb, lambda g: tT[:, g * 4:(g + 1) * 4, :], b_tT, bank0=0)
                  for n in range(4):
                      for kc in range(KC):
                          kb.op(pe, L("matmul", psf[4 + n][:, 0:512], lhsT=tT[:, kc, :], rhs=wbig[:, kc, n * 512:(n + 1) * 512],
                                      start=(kc == 0), stop=(kc == KC - 1)), reads=[b_tT, b_wb[n]], writes=[b_ps[4 + n]],
                                inc=(kc == KC - 1))
                      kb.op(act, L("copy", out=xb[:, n * 512:(n + 1) * 512], in_=psf[4 + n][:, 0:512]), reads=[b_ps[4 + n]],
                            writes=[b_xb])
                  transpose16(xb, b_xb, lambda g: qTc[:, g * 4:(g + 1) * 4, :], b_qTc, bank0=2)
                  for cg in range(4):
                      bk = 4 + (cg % 2)
                      for cc in range(4):
                          c = cg * 4 + cc
                          kb.op(pe, L("matmul", psf[bk][:, cc * 128:(cc + 1) * 128], lhsT=qTc[:, c, :], rhs=skb[:, c, :],
                                      start=True, stop=True), reads=[b_qTc, b_skb], writes=[b_ps[bk]], inc=(cc == 3))
                      kb.op(act, L("copy", out=sc[:, cg * 4:(cg + 1) * 4, :],
                                   in_=psf[bk][:, 0:512].rearrange("p (c t) -> p c t", c=4)), reads=[b_ps[bk]], writes=[b_sc])
                  for c in range(16):
                      kb.op(dve, L("max", out=top[:, c, 0:8], in_=sc[:, c, :]), reads=[b_sc], writes=[b_top])
                      kb.op(dve, L("max_index", out=ti[:, c, 0:8], in_max=top[:, c, 0:8], in_values=sc[:, c, :]),
                            reads=[b_sc, b_top], writes=[b_ti])
                      kb.op(dve, L("match_replace", out=sc2[:, c, :], in_to_replace=top[:, c, 0:8], in_values=sc[:, c, :],
                                   imm_value=NEGBIG), reads=[b_sc, b_top], writes=[b_sc2])
                      kb.op(dve, L("max", out=top[:, c, 8:16], in_=sc2[:, c, :]), reads=[b_sc2], writes=[b_top])
                      kb.op(dve, L("max_index", out=ti[:, c, 8:16], in_max=top[:, c, 8:16], in_values=sc2[:, c, :]),
                            reads=[b_sc2, b_top], writes=[b_ti])
                  kb.op(dve, L("tensor_copy", out=tif, in_=ti), reads=[b_ti], writes=[b_ti])
                  top4 = top.rearrange("p (h two) k -> p h two k", two=2)
                  tif4 = tif.rearrange("p (h two) k -> p h two k", two=2)
                  kb.op(dve, L("tensor_scalar", out=tif4[:, :, 0, :], in0=tif4[:, :, 0, :], scalar1=128.0, scalar2=None,
                               op0=ALU.mult), reads=[b_ti], writes=[b_ti])
                  cs4 = cand_s.rearrange("p h (a b) -> p h a b", a=16)
                  ci4 = cand_i.rearrange("p h (a b) -> p h a b", a=16)
                  bc = [128, 8, 16, 16]
                  kb.op(dve, L("tensor_tensor", out=cs4, in0=top4[:, :, 0, :].unsqueeze(3).broadcast_to(bc),
                               in1=top4[:, :, 1, :].unsqueeze(2).broadcast_to(bc), op=ALU.add), reads=[b_top], writes=[b_cs_])
                  for h in range(8):
                      kb.op(dve, L("max", out=best[:, h, 0:8], in_=cand_s[:, h, :]), reads=[b_cs_], writes=[b_best])
                      kb.op(dve, L("max_index", out=posu[:, h, 0:8], in_max=best[:, h, 0:8], in_values=cand_s[:, h, :]),
                            reads=[b_cs_, b_best], writes=[b_pos])
                      kb.op(dve, L("match_replace", out=cand2[:, h, :], in_to_replace=best[:, h, 0:8],
                                   in_values=cand_s[:, h, :], imm_value=NEGBIG), reads=[b_cs_, b_best], writes=[b_sc2])
                      kb.op(dve, L("max", out=best[:, h, 8:16], in_=cand2[:, h, :]), reads=[b_sc2], writes=[b_best])
                      kb.op(dve, L("max_index", out=posu[:, h, 8:16], in_max=best[:, h, 8:16], in_values=cand2[:, h, :]),
                            reads=[b_sc2, b_best], writes=[b_pos])
                  P = [b_pos]
                  eq4 = eqb.rearrange("p (h k a) -> p h k a", h=8, k=16)
                  eq3 = eqb.rearrange("p (n a) -> p n a", a=16)
                  kb.op(dve, L("tensor_copy", out=posf, in_=posu), reads=P, writes=P)
                  kb.op(dve, L("tensor_tensor", out=eq4, in0=posf.unsqueeze(3).broadcast_to(bc),
                               in1=lo16.unsqueeze(1).unsqueeze(1).broadcast_to(bc), op=ALU.is_ge), reads=P + [b_const],
                        writes=[b_sc])
                  kb.op(dve, L("tensor_reduce", out=paf.rearrange("p h k -> p (h k)"), in_=eq3, axis=AX.X, op=ALU.add),
                        reads=[b_sc], writes=P)
                  kb.op(dve, L("tensor_scalar", out=paf, in0=paf, scalar1=-1.0, scalar2=None, op0=ALU.add), reads=P, writes=P)
                  kb.op(dve, L("scalar_tensor_tensor", out=pbf, in0=paf, scalar=-16.0, in1=posf, op0=ALU.mult, op1=ALU.add),
                        reads=P, writes=P)
                  io4 = iota16.unsqueeze(1).unsqueeze(1).broadcast_to(bc)
                  for (pf, side, dsel) in ((paf, 0, idxf), (pbf, 1, i2sel)):
                      kb.op(dve, L("tensor_tensor", out=eq4, in0=pf.unsqueeze(3).broadcast_to(bc), in1=io4, op=ALU.is_equal),
                            reads=P + [b_const], writes=[b_sc])
                      kb.op(dve, L("tensor_tensor", out=eq4, in0=eq4, in1=tif4[:, :, side, :].unsqueeze(2).broadcast_to(bc),
                                   op=ALU.mult), reads=[b_sc, b_ti], writes=[b_sc])
                      kb.op(dve, L("tensor_reduce", out=dsel, in_=eq3, axis=AX.X, op=ALU.add), reads=[b_sc], writes=[b_idxf])
                  kb.op(dve, L("tensor_tensor", out=idxf, in0=idxf, in1=i2sel, op=ALU.add), reads=[b_idxf], writes=[b_idxf])
                  kb.op(dve, L("tensor_scalar", out=idxf, in0=idxf, scalar1=16383.0, scalar2=None, op0=ALU.min),
                        reads=[b_idxf], writes=[b_idxf])
                  kb.op(dve, L("tensor_copy", out=idx_all[:, bl, :], in_=idxf), reads=[b_idxf], writes=[b_idx])
                  kb.op(dve, L("tensor_tensor", out=exb, in0=best, in1=best[:, :, 0:1].broadcast_to([128, 8, 16]),
                               op=ALU.subtract), reads=[b_best], writes=[b_ex])
                  kb.op(act, L("activation", out=exb, in_=exb, func=AF.Exp), reads=[b_ex], writes=[b_ex])
                  kb.op(dve, L("tensor_reduce", out=smb[:, 0:8], in_=exb, axis=AX.X, op=ALU.add), reads=[b_ex], writes=[b_ex])
                  kb.op(dve, L("reciprocal", out=smb[:, 8:16], in_=smb[:, 0:8]), reads=[b_ex], writes=[b_ex])
                  kb.op(dve, L("tensor_tensor", out=gate_all[:, bl, :].rearrange("p (h k) -> p h k", h=8), in0=exb,
                               in1=smb[:, 8:16].unsqueeze(2).broadcast_to([128, 8, 16]), op=ALU.mult), reads=[b_ex],
                        writes=[b_gate])
              if s == 0 and half == 0:
                  tap("idx", idx_all[:, 0:8, :].rearrange("p b k -> p (b k)"), [b_idx])
                  tap("gate", gate_all[:, 0:8, :].rearrange("p b k -> p (b k)"), [b_gate])
              if stage < 6:
                  continue
              kb.barrier()
              gbuf = [carve(o_big + 2048 * q, 2048, BF16) for q in range(8)]
              b_gb = [B("gb%d" % q) for q in range(8)]
              hnb = [carve(o_r1, 2048), carve(o_r1 + 2048, 2048)]
              b_hn = [B("hn%d" % q) for q in range(2)]
              zbuf = carve(o_r1 + 4096, 2048)
              outt = carve(o_r1 + 6144, 2048)
              b_z, b_outt = B("zb"), B("outt")
              dg = [carve(o_r1 + 8192 + 64 * q, 64, BF16) for q in range(4)]
              b_dg = [B("dg%d" % q) for q in range(4)]
              arb = [carve(o_r3, 128), carve(o_r3 + 128, 128)]
              glb = [carve(o_r3 + 256, 128), carve(o_r3 + 384, 128)]
              b_ar = [B("ar%d" % q) for q in range(8)]
              b_gl = [B("gl%d" % q) for q in range(8)]
              for bl in range(16):
                  pu = bl % 2
                  ru = bl * 128
                  kb.dma(sp, L("dma_start", out=xt[pu], in_=xmidd[s, ru:ru + 128, :]), writes=[b_xt[pu]], owner=b_xt[pu])
                  rmsnorm_to(xt[pu], b_xt[pu], nrm, b_nrm, hnb[pu], b_hn[pu])

                  def vstep(k):
                      q = k % 8
                      d4 = k % 4
                      kb.op(dve, L("tensor_scalar", out=dg[d4], in0=identb, scalar1=glb[pu][:, k:k + 1],
                                   scalar2=gate_all[:, bl, k:k + 1], op0=ALU.mult, op1=ALU.mult),
                            reads=[b_const, b_gl[q], b_gate], writes=[b_dg[d4]])
                      for n in range(4):
                          kb.op(pe, L("matmul", psf[4 + n][:, 0:512], lhsT=dg[d4],
                                      rhs=gbuf[q][:, D + n * 512:D + (n + 1) * 512], start=(k == 0), stop=(k == 127)),
                                reads=[b_dg[d4], b_gb[q]], writes=[b_ps[4 + n]], inc=(n == 3))

                  for k in range(128):
                      q = k % 8
                      kb.dma(pool, L("indirect_dma_start", out=gbuf[q], out_offset=None, in_=uvd,
                                     in_offset=bass.IndirectOffsetOnAxis(ap=idx_all[:, bl, k:k + 1], axis=0)),
                             reads=[b_idx], writes=[b_gb[q]], owner=b_gb[q])
                      kb.op(dve, L("scalar_tensor_tensor", out=xb, in0=hnb[pu], scalar=1.0, in1=gbuf[q][:, 0:D], op0=ALU.mult,
                                   op1=ALU.mult, accum_out=arb[pu][:, k:k + 1]), reads=[b_hn[pu], b_gb[q]],
                            writes=[b_xb, b_ar[q]])
                      kb.op(act, L("activation", out=glb[pu][:, k:k + 1], in_=arb[pu][:, k:k + 1], func=AF.Gelu),
                            reads=[b_ar[q]], writes=[b_gl[q]])
                      if k >= 1:
                          vstep(k - 1)
                  vstep(127)
                  for n in range(4):
                      kb.op(dve, L("tensor_tensor", out=zbuf[:, n * 512:(n + 1) * 512], in0=psf[4 + n][:, 0:512],
                                   in1=xt[pu][:, n * 512:(n + 1) * 512], op=ALU.add), reads=[b_ps[4 + n], b_xt[pu]],
                            writes=[b_z])
                  rmsnorm_to(zbuf, b_z, nrm2, b_nrm2, outt, b_outt)
                  kb.dma(sp, L("dma_start", out=out[s, ru:ru + 128, :], in_=outt), reads=[b_outt], writes=[],
                         owner=b_outt)
      except StopBuild:
        break

    kb.barrier()
    kb.emit()
    return nc


def host_consts():
    pos = np.maximum(np.arange(T) - 112, 0).astype(np.float32)
    inv = (np.float32(10000.0) ** (-np.arange(0, 64, 2, dtype=np.float32) / np.float32(64))).astype(np.float32)
    ang = pos[:, None] * inv[None, :]
    ang = np.concatenate([ang, ang], axis=-1)
    ang = np.concatenate([ang, ang], axis=-1).T
    cosd = np.cos(ang).astype(np.float32)
    sind = np.sin(ang).astype(np.float32)
    cm = np.zeros((4, 128, 128), np.float32)
    cm[0] = np.eye(128, dtype=np.float32)
    for po in range(128):
        base = (po // 64) * 64
        d = po % 64
        if d < 32:
            cm[1, base + d + 32, po] = -1.0
        else:
            cm[1, base + d - 32, po] = 1.0
    kk, qq = np.meshgrid(np.arange(128), np.arange(128), indexing="ij")
    cm[2] = np.where(kk > qq, -30000.0, 0.0)
    cm[3] = np.arange(128, dtype=np.float32)[None, :]
    cm[3][:, 16:32] = 16.0 * np.arange(16, dtype=np.float32)[None, :]
    validc = np.ones((128, NB), np.float32)
    validc[0:112, 0] = 0.0
    return dict(cosd=np.ascontiguousarray(cosd), sind=np.ascontiguousarray(sind), cmat=cm, validc=validc)


def host_layout(inp):
    f = lambda a: np.ascontiguousarray(np.asarray(a, dtype=np.float32))
    c = host_consts()
    c["nrm3"] = f(np.stack([np.broadcast_to(np.asarray(inp[k]).reshape(1, D), (128, D))
                            for k in ("norm_mix_w", "norm_ffn_w", "norm_final_w")]))
    c["w_in"] = f(inp["w_in"][0])
    c["w_out"] = f(inp["w_out"][0])
    c["w_q"] = f(inp["peer_w_q"][0])
    c["subln"] = f(np.broadcast_to(np.asarray(inp["attn_subln_w"][0]).reshape(1, 128), (128, 128)))
    c["mnw"] = f(np.broadcast_to(np.asarray(inp["mlstm_norm_w"][0]).reshape(1, 1024), (128, 1024)))
    c["convw"] = f(np.asarray(inp["mlstm_conv_w"][0]).reshape(4, 8, 128).transpose(2, 1, 0))
    c["convb"] = f(np.asarray(inp["mlstm_conv_b"][0]).reshape(8, 128).T)
    c["gbias"] = f(np.stack([np.asarray(inp["mlstm_i_b"][0]), np.asarray(inp["mlstm_f_b"][0])], axis=1))
    c["lamqk"] = f(np.asarray(inp["attn_lambda_qk"][0]).reshape(1, 256))
    c["skT"] = f(np.asarray(inp["peer_sub_keys"][0]).reshape(16, 128, 128).transpose(2, 0, 1))
    return c


def host_xp(inp, core):
    x = np.asarray(inp["x"], dtype=np.float32)
    meta = np.asarray(inp["meta_tokens"], dtype=np.float32)
    xp = np.zeros((NSEQ, T, D), np.float32)
    for s in range(NSEQ):
        xp[s, 112:128] = meta
        xp[s, 128:] = x[core * NSEQ + s]
    return xp


_CACHE = {}


def kernel(**inp):
    if "nc" not in _CACHE:
        _CACHE["nc"] = build()
    nc = _CACHE["nc"]
    c = host_layout(inp)
    c["peer_u"] = np.ascontiguousarray(np.asarray(inp["peer_u"][0], dtype=np.float32))
    c["peer_v"] = np.ascontiguousarray(np.asarray(inp["peer_v"][0], dtype=np.float32))
    in_maps = []
    for core in range(NCORES):
        m = dict(c)
        m["xp"] = host_xp(inp, core)
        in_maps.append(m)
    res = run_bass_kernel_spmd(nc, in_maps, core_ids=list(range(NCORES)))
    outs = [np.asarray(r["out"]).reshape(NSEQ, 2048, D) for r in res.results]
    return np.concatenate(outs, axis=0).astype(np.float32)
```

```python
import numpy as np
from contextlib import ExitStack
import concourse.bass as bass
import concourse.mybir as mybir
from concourse.bass_utils import run_bass_kernel_spmd

F32 = mybir.dt.float32
BF16 = mybir.dt.bfloat16
I32 = mybir.dt.int32
U32 = mybir.dt.uint32
AF = mybir.ActivationFunctionType
ALU = mybir.AluOpType
AX = mybir.AxisListType

D = 2048
KC = 16
T = 2176
NB = 17
NSEQ = 2
NCORES = 8
EPS = 1e-6
C_AQ, C_AK, C_AV, C_MQ, C_MK, C_MV, C_MO, C_MI, C_MF = 0, 1024, 2048, 3072, 3584, 4096, 5120, 6144, 6148
INC = 6152
TG = [(0, 128), (128, 512), (640, 512), (1152, 512), (1664, 512)]
NEGBIG = -1.0e30
LAMBDA_INIT = 0.2


def L(method, *a, **k):
    return lambda e: getattr(e, method)(*a, **k)


class StopBuild(Exception):
    pass


import os
CUT = int(os.environ.get("KCUT", "0"))


def cut(n):
    if CUT == n:
        raise StopBuild()


class Buf:
    def __init__(self, name):
        self.name = name
        self.lw = None
        self.rd = {}
        self.dsem = None
        self.dcnt = 0


class Eng:
    def __init__(self, name, sem, is_pe=False):
        self.name = name
        self.sem = sem
        self.cnt = 0
        self.ops = []
        self.waited = {}
        self.is_pe = is_pe


class KB:
    def __init__(self, nc, es):
        self.nc = nc
        self.es = es
        self.pe = Eng("pe", es.enter_context(nc.semaphore("s_pe")), True)
        self.act = Eng("act", es.enter_context(nc.semaphore("s_act")))
        self.dve = Eng("dve", es.enter_context(nc.semaphore("s_dve")))
        self.pool = Eng("pool", es.enter_context(nc.semaphore("s_pool")))
        self.sp = Eng("sp", None)
        self.engs = [self.pe, self.act, self.dve, self.pool, self.sp]
        self.bufs = []
        self.bycname = {}
        self.nsem = 4

    def buf(self, name):
        if name in self.bycname:
            return self.bycname[name]
        b = Buf(name)
        self.bufs.append(b)
        self.bycname[name] = b
        return b

    def _need(self, eng, tok):
        sem, val, src = tok
        if src is eng and eng.is_pe:
            return
        key = id(sem)
        if eng.waited.get(key, 0) >= val:
            return
        eng.waited[key] = val
        eng.ops.append(L("wait_ge", sem, val))

    def _deps(self, eng, reads, writes):
        for b in reads:
            if b.lw is not None:
                self._need(eng, b.lw)
        for b in writes:
            if b.lw is not None:
                self._need(eng, b.lw)
            for t in b.rd.values():
                self._need(eng, t)

    def _commit(self, tok, reads, writes):
        for b in reads:
            b.rd[id(tok[0])] = tok
        for b in writes:
            b.lw = tok
            b.rd = {}

    def op(self, eng, fn, reads=(), writes=(), inc=True):
        ex = [b for b in reads if getattr(b, "excl", False) and b not in writes]
        if ex:
            writes = list(writes) + ex
        self._deps(eng, reads, writes)
        if inc:
            eng.cnt += 1
            tok = (eng.sem, eng.cnt, eng)
            eng.ops.append(lambda e, fn=fn, sem=eng.sem: fn(e).then_inc(sem, 1))
        else:
            tok = (eng.sem, eng.cnt + 1, eng)
            eng.ops.append(fn)
        self._commit(tok, reads, writes)

    def dma(self, q, fn, reads=(), writes=(), owner=None):
        assert owner is not None
        if owner.dsem is None:
            owner.dsem = self.es.enter_context(self.nc.semaphore("d_" + owner.name))
            self.nsem += 1
        self._deps(q, reads, writes)
        owner.dcnt += 1
        tok = (owner.dsem, 16 * owner.dcnt, None)
        q.ops.append(lambda e, fn=fn, sem=owner.dsem: fn(e).then_inc(sem, 16))
        self._commit(tok, reads, writes)

    def barrier(self):
        toks = []
        for e in (self.pe, self.act, self.dve, self.pool):
            if e.cnt:
                toks.append((e.sem, e.cnt, None))
        for b in self.bufs:
            if b.dsem is not None and b.dcnt:
                toks.append((b.dsem, 16 * b.dcnt, None))
        for e in self.engs:
            for t in toks:
                if e.sem is not None and t[0] is e.sem:
                    continue
                self._need(e, t)
        for b in self.bufs:
            b.lw = None
            b.rd = {}

    def emit(self):
        nc = self.nc
        with nc.Block() as block:
            @block.sync
            def _(e):
                for f in self.sp.ops:
                    f(e)

            @block.scalar
            def _(e):
                for f in self.act.ops:
                    f(e)

            @block.tensor
            def _(e):
                for f in self.pe.ops:
                    f(e)

            @block.vector
            def _(e):
                for f in self.dve.ops:
                    f(e)

            @block.gpsimd
            def _(e):
                for f in self.pool.ops:
                    f(e)


def build(stage=99, nseq=NSEQ, taps=None):
    nc = bass.Bass("TRN2", target_bir_lowering=False)
    es = ExitStack()
    dt_in = lambda name, shape, dt=F32: nc.dram_tensor(name, shape, dt, kind="ExternalInput").ap()
    xp = dt_in("xp", [NSEQ, T, D])
    nrm3 = dt_in("nrm3", [3, 128, D])
    w_in = dt_in("w_in", [D, INC])
    w_out = dt_in("w_out", [D, D])
    w_q = dt_in("w_q", [D, D])
    cosd = dt_in("cosd", [128, T])
    sind = dt_in("sind", [128, T])
    cmat = dt_in("cmat", [4, 128, 128])
    subln = dt_in("subln", [128, 128])
    mnw = dt_in("mnw", [128, 1024])
    convw = dt_in("convw", [128, 8, 4])
    convb = dt_in("convb", [128, 8])
    gbias = dt_in("gbias", [4, 2])
    lamqk = dt_in("lamqk", [1, 256])
    validc = dt_in("validc", [128, NB])
    skT = dt_in("skT", [128, 16, 128])
    full = stage >= 6
    if full:
        peer_u = dt_in("peer_u", [16384, D])
        peer_v = dt_in("peer_v", [16384, D])
    out = nc.dram_tensor("out", [NSEQ, 2048, D], F32, kind="ExternalOutput").ap()
    mixd = nc.dram_tensor("mixd", [NSEQ, 2048, D], BF16, kind="Internal").ap()
    xmidd = nc.dram_tensor("xmidd", [NSEQ, 2048, D], F32, kind="Internal").ap()
    gsc = nc.dram_tensor("gsc", [NSEQ, 2, 4, T], F32, kind="Internal").ap()
    if full:
        uvd = nc.dram_tensor("uvd", [16384, 2 * D], BF16, kind="Internal").ap()
    tapd = {}
    if taps:
        for name, shape in taps.items():
            tapd[name] = nc.dram_tensor("tap_" + name, shape, F32, kind="ExternalOutput").ap()

    AR = es.enter_context(nc.sbuf_tensor("arena", [128, 51200], F32))
    PS = [es.enter_context(nc.psum_tensor("ps%d" % i, [128, 512], F32)) for i in range(8)]
    kb = KB(nc, es)
    pe, act, dve, pool, sp = kb.pe, kb.act, kb.dve, kb.pool, kb.sp

    def carve(off, n, dt=F32, pat=None, parts=128, **kw):
        ap = AR[0:parts, off:off + n]
        if dt != F32:
            ap = ap.bitcast(dt)
        if pat:
            ap = ap.rearrange(pat, **kw)
        return ap

    o = 0

    def alloc(n):
        nonlocal o
        r = o
        o += n
        return r

    o_big = alloc(17408)
    o_qk = alloc(2176)
    o_va = alloc(2192)
    o_pre = alloc(2180)
    o_ycv = alloc(2176)
    o_sg = alloc(2048)
    o_crow = alloc(2176)
    o_nmrow = alloc(2176)
    o_e = alloc(512)
    o_dm = alloc(512)
    o_qbf = alloc(256)
    o_slab = alloc(4096)
    o_cos = alloc(2176)
    o_sin = alloc(2176)
    o_c = alloc(3000)
    o_tT = alloc(1024)
    o_misc = alloc(1024)
    assert o <= 51200, o

    oc = o_c
    identb = carve(oc, 64, BF16); oc += 64
    rmb = carve(oc, 64, BF16); oc += 64
    trib = carve(oc, 64, BF16); oc += 64
    identf = carve(oc, 128); oc += 128
    subln8 = carve(oc, 128); oc += 128
    mnwt = carve(oc, 1024); oc += 1024
    cwt = carve(oc, 32, F32, "p (c j) -> p c j", j=4); oc += 32
    cbt = carve(oc, 8); oc += 8
    gbt = carve(oc, 2, parts=4); oc += 2
    nfb = carve(oc, 1, parts=4); oc += 1
    vct = carve(oc, NB); oc += NB
    neglam = carve(oc, 1); oc += 1
    lamrow = carve(oc, 256, parts=1); oc += 256
    lamtmp = carve(oc, 8, parts=1); oc += 8
    onesrow = carve(oc, 512, parts=1); oc += 512
    ones4 = None
    cstage = carve(oc, 128); oc += 128
    iota16 = carve(oc, 16); oc += 16
    lo16 = carve(oc, 16); oc += 16
    assert oc <= o_c + 3000

    B = kb.buf
    b_const = B("const")

    xnT = carve(o_big, 17408, BF16, "p (k t) -> p k t", k=KC)
    b_xnT = B("xnT")
    qT = carve(o_qk, 1088, BF16)
    kT = carve(o_qk + 1088, 1088, BF16)
    b_qT, b_kT = B("qT"), B("kT")
    vaug = carve(o_va, 2192, BF16)[:, 0:NB * 257].rearrange("p (b e) -> p b e", e=257)
    b_vaug = [B("vaug%d" % i) for i in range(NB)]
    pre = carve(o_pre, 2180)
    ycv = carve(o_ycv, 2176)
    b_pre, b_ycv = B("pre"), B("ycv")
    crow = carve(o_crow, 2176, parts=2)
    nmrow = carve(o_nmrow, 2176, parts=2)
    b_crow, b_nmrow = B("crow"), B("nmrow")
    ebuf = [carve(o_e + 256 * i, 256, BF16) for i in range(2)]
    b_e = [B("e0"), B("e1")]
    dmb = carve(o_dm, 512)
    b_dm = B("dm")
    qbf = carve(o_qbf, 256, BF16)
    b_qbf = B("qbf")
    slab = [carve(o_slab + 1024 * i, 1024, BF16, "p (k c) -> p k c", k=KC) for i in range(4)]
    b_slab = [B("slab%d" % i) for i in range(4)]
    cost = carve(o_cos, 2176)
    sint = carve(o_sin, 2176)
    tT = carve(o_tT, 1024, BF16, "p (k t) -> p k t", k=KC)
    b_tT = B("tT")
    b_ps = [B("ps%d" % i) for i in range(8)]
    for _b in b_ps:
        _b.excl = True
    psb = [PS[i].ap().bitcast(BF16) for i in range(8)]
    psf = [PS[i].ap() for i in range(8)]

    om = o_misc
    ss = carve(om, 8); om += 8
    b_ss = B("ss")
    small = carve(om, 64); om += 64
    b_small = B("small")
    emtT = carve(om, 64); om += 64
    b_emtT = B("emtT")
    wg = carve(om, 64, BF16, "p (k c) -> p k c", k=KC); om += 64
    b_wg = B("wg")
    mixst = carve(om, 128, BF16); om += 128
    b_mixst = B("mixst")

    o_p1 = o_qk
    xt = [carve(o_p1, 2048), carve(o_p1 + 2048, 2048)]
    xb = carve(o_p1 + 4096, 1024, BF16)
    nrm = carve(o_p1 + 5120, 2048)
    nrm2 = carve(o_p1 + 7168, 2048)
    b_xt = [B("xt0"), B("xt1")]
    b_xb = B("xb")
    b_nrm = B("nrm")
    b_nrm2 = B("nrm2")

    def tap(name, src_ap, bufs, parts=128):
        if name in tapd:
            bt = B("tap_" + name)
            kb.barrier()
            kb.dma(pool, L("dma_start", out=tapd[name], in_=src_ap, max_dma_last_dim=2048), reads=bufs,
                   writes=[bt], owner=bt)
            kb.tapbufs.append(bt)
    kb.tapbufs = []

    def ld(dst, src, b=b_const, q=sp):
        kb.dma(q, L("dma_start", out=dst, in_=src), writes=[b], owner=b)

    ld(mnwt, mnw)
    ld(cwt, convw)
    ld(cbt, convb)
    ld(gbt, gbias)
    ld(vct, validc)
    ld(lamrow, lamqk)
    ld(subln8, subln)
    ld(identf, cmat[0])
    ld(iota16, cmat[3][:, 0:16])
    ld(lo16, cmat[3][:, 16:32])
    b_cst = B("cstage")
    for i, dstb in enumerate([identb, rmb, trib]):
        kb.dma(sp, L("dma_start", out=cstage, in_=cmat[i]), writes=[b_cst], owner=b_cst)
        kb.op(dve, L("tensor_copy", out=dstb, in_=cstage), reads=[b_cst], writes=[b_const])
    kb.op(dve, L("tensor_scalar", out=subln8, in0=subln8, scalar1=1.0 - LAMBDA_INIT, scalar2=None, op0=ALU.mult),
          reads=[b_const], writes=[b_const])
    kb.op(dve, L("tensor_scalar", out=nfb, in0=gbt[:, 1:2], scalar1=-1.0, scalar2=None, op0=ALU.mult),
          reads=[b_const], writes=[b_const])
    kb.op(dve, L("memset", onesrow, 1.0), writes=[b_const])
    kb.op(dve, L("memset", lamtmp, 0.0), writes=[b_const])
    kb.op(dve, L("scalar_tensor_tensor", out=lamrow[:, 0:64], in0=lamrow[:, 0:64], scalar=1.0, in1=lamrow[:, 64:128],
                                                op0=ALU.mult, op1=ALU.mult, accum_out=lamtmp[:, 0:1]),
          reads=[b_const], writes=[b_const])
    kb.op(dve, L("scalar_tensor_tensor", out=lamrow[:, 128:192], in0=lamrow[:, 128:192], scalar=1.0,
                                                in1=lamrow[:, 192:256], op0=ALU.mult, op1=ALU.mult,
                                                accum_out=lamtmp[:, 1:2]),
          reads=[b_const], writes=[b_const])
    kb.op(act, L("activation", out=lamtmp[:, 2:4], in_=lamtmp[:, 0:2], func=AF.Exp), reads=[b_const], writes=[b_const])
    kb.op(dve, L("tensor_tensor", out=lamtmp[:, 4:5], in0=lamtmp[:, 3:4], in1=lamtmp[:, 2:3], op=ALU.subtract),
          reads=[b_const], writes=[b_const])
    kb.op(dve, L("tensor_scalar", out=lamtmp[:, 5:6], in0=lamtmp[:, 4:5], scalar1=-LAMBDA_INIT, scalar2=None,
                                         op0=ALU.add), reads=[b_const], writes=[b_const])
    kb.op(pe, L("matmul", psf[0][:, 0:8], lhsT=onesrow[:, 0:128], rhs=lamtmp[:, 0:8], start=True, stop=True),
          reads=[b_const], writes=[b_ps[0]])
    kb.op(act, L("copy", out=neglam, in_=psf[0][:, 5:6]), reads=[b_ps[0]], writes=[b_const])

    def rmsnorm_to(xin, b_xin, nrmt, b_nrmt, dst, b_dst, d=D):
        kb.op(act, L("activation", out=dst, in_=xin, func=AF.Square, accum_out=ss[:, 0:1]),
              reads=[b_xin], writes=[b_dst, b_ss])
        kb.op(act, L("activation", out=ss[:, 1:2], in_=ss[:, 0:1], func=AF.Sqrt, scale=1.0 / d, bias=EPS),
              reads=[b_ss], writes=[b_ss])
        kb.op(dve, L("reciprocal", out=ss[:, 2:3], in_=ss[:, 1:2]), reads=[b_ss], writes=[b_ss])
        kb.op(dve, L("scalar_tensor_tensor", out=dst, in0=xin, scalar=ss[:, 2:3], in1=nrmt, op0=ALU.mult,
                                                    op1=ALU.mult), reads=[b_xin, b_ss, b_nrmt], writes=[b_dst])

    def transpose16(src, b_src, dstfn, b_dst, bank0=0):
        for g in range(4):
            bk = bank0 + (g % 2)
            for j in range(4):
                kc = g * 4 + j
                kb.op(pe, L("transpose", out=psb[bk][:, j * 128:(j + 1) * 128],
                                                                    in_=src[:, kc * 128:(kc + 1) * 128], identity=identb),
                      reads=[b_src, b_const], writes=[b_ps[bk]], inc=(j == 3))
            kb.op(act, L("copy", out=dstfn(g), in_=psb[bk][:, 0:512].rearrange("p (k t) -> p k t", k=4)),
                  reads=[b_ps[bk]], writes=[b_dst])

    def load_slab(slot, col0, ncols=128, src=None):
        src = w_in if src is None else src
        kb.dma(pool, L("dma_start",
            out=slab[slot][:, :, 0:ncols], in_=src[:, col0:col0 + ncols].rearrange("(k p) c -> p k c", p=128)),
            writes=[b_slab[slot]], owner=b_slab[slot])

    def proj_fm(slot, ncols, tok0, ntok, bank, lhs=None, b_lhs=None):
        for kc in range(KC):
            l = slab[slot][:, kc, 0:ncols] if lhs is None else lhs(kc)
            kb.op(pe, L("matmul", psf[bank][0:ncols, 0:ntok], lhsT=l, rhs=xnT[:, kc, tok0:tok0 + ntok],
                                                     start=(kc == 0), stop=(kc == KC - 1)),
                  reads=[b_slab[slot] if b_lhs is None else b_lhs, b_xnT], writes=[b_ps[bank]], inc=(kc == KC - 1))

    slab2 = [carve(o_slab + 2048 * i, 2048, BF16, "p (k c) -> p k c", k=KC) for i in range(2)]

    def load_slab256(pair, col0):
        kb.dma(pool, L("dma_start", out=slab2[pair], in_=w_in[:, col0:col0 + 256].rearrange("(k p) c -> p k c", p=128)),
               writes=[b_slab[2 * pair], b_slab[2 * pair + 1]], owner=b_slab[2 * pair])

    def proj_tm256(pair, blk, bank):
        for kc in range(KC):
            kb.op(pe, L("matmul", psf[bank][:, 0:256], lhsT=xnT[:, kc, blk * 128:(blk + 1) * 128], rhs=slab2[pair][:, kc, :],
                        start=(kc == 0), stop=(kc == KC - 1)),
                  reads=[b_slab[2 * pair], b_slab[2 * pair + 1], b_xnT], writes=[b_ps[bank]], inc=(kc == KC - 1))

    def proj_tm(slots, blk, bank):
        for hi, slot in enumerate(slots):
            for kc in range(KC):
                kb.op(pe, L("matmul",
                    psf[bank][:, hi * 128:(hi + 1) * 128], lhsT=xnT[:, kc, blk * 128:(blk + 1) * 128],
                    rhs=slab[slot][:, kc, :], start=(kc == 0), stop=(kc == KC - 1)),
                    reads=[b_slab[slot], b_xnT], writes=[b_ps[bank]], inc=(kc == KC - 1 and hi == len(slots) - 1))

    if full:
        cf = [carve(o_big + 2048 * i, 2048) for i in range(4)]
        cbf = [carve(o_big + 8192 + 1024 * i, 1024, BF16) for i in range(4)]
        b_cf = [B("cf%d" % i) for i in range(4)]
        b_cbf = [B("cbf%d" % i) for i in range(4)]
        n = 0
        for (src, dstd) in ((peer_u, uvd[:, 0:D]), (peer_v, uvd[:, D:2 * D])):
            for i in range(128):
                q = n % 4
                kb.dma(sp, L("dma_start", out=cf[q], in_=src[i * 128:(i + 1) * 128, :]), writes=[b_cf[q]], owner=b_cf[q])
                if n % 2:
                    kb.op(act, L("copy", out=cbf[q], in_=cf[q]), reads=[b_cf[q]], writes=[b_cbf[q]])
                else:
                    kb.op(dve, L("tensor_copy", out=cbf[q], in_=cf[q]), reads=[b_cf[q]], writes=[b_cbf[q]])
                kb.dma(pool, L("dma_start", out=dstd[i * 128:(i + 1) * 128, :], in_=cbf[q]), reads=[b_cbf[q]], writes=[],
                       owner=b_cbf[q])
                n += 1

    for s in range(nseq):
      try:
          kb.barrier()
          kb.dma(sp, L("dma_start", out=nrm, in_=nrm3[0]), writes=[b_nrm], owner=b_nrm)
          for b in range(NB):
              x = xt[b % 2]
              bx = b_xt[b % 2]
              kb.dma(sp, L("dma_start", out=x, in_=xp[s, b * 128:(b + 1) * 128, :]), writes=[bx], owner=bx)
              rmsnorm_to(x, bx, nrm, b_nrm, xb, b_xb)
              transpose16(xb, b_xb, lambda g, b=b: xnT[:, g * 4:(g + 1) * 4, b * 128:(b + 1) * 128], b_xnT)
          if s == 0:
              tap("xnT", xnT[:, 0, :], [b_xnT])
          if stage < 2:
              continue
          kb.barrier()

          b_cs = B("cossin")
          kb.dma(sp, L("dma_start", out=cost, in_=cosd), writes=[b_cs], owner=b_cs)
          b_cs2 = B("cossin2")
          kb.dma(sp, L("dma_start", out=sint, in_=sind), writes=[b_cs2], owner=b_cs2)
          kb.barrier()
          for h in range(8):
              load_slab(0, C_AQ + h * 128)
              load_slab(1, C_AK + h * 128)
              load_slab(2, C_AV + h * 128)
              if h == 0: cut(1)
              qbfs = [qbf, carve(o_sg + 1536, 256, BF16)]
              b_qbfs = [b_qbf, B("qbf2")]
              b_pres = [b_pre, B("pre2")]
              b_ycvs = [b_ycv, B("ycv2")]
              chn = 0
              for (tok0, ntok) in TG:
                  for which in range(2):
                      dst, b_dst = (qT, b_qT) if which == 0 else (kT, b_kT)
                      cp = chn % 2
                      chn += 1
                      pbk, rbk = (0, 1) if cp == 0 else (2, 3)
                      qb, b_qb = qbfs[cp], b_qbfs[cp]
                      pr = pre[:, cp * 1024:cp * 1024 + ntok]
                      yc = ycv[:, cp * 1024:cp * 1024 + ntok]
                      proj_fm(which, 128, tok0, ntok, pbk)
                      kb.op(act, L("copy", out=qb[:, 0:ntok], in_=psf[pbk][:, 0:ntok]), reads=[b_ps[pbk]], writes=[b_qb])
                      kb.op(dve, L("tensor_tensor", out=pr, in0=psf[pbk][:, 0:ntok], in1=cost[:, tok0:tok0 + ntok], op=ALU.mult),
                            reads=[b_ps[pbk], b_const], writes=[b_pres[cp]])
                      kb.op(pe, L("matmul", psf[rbk][:, 0:ntok], lhsT=rmb, rhs=qb[:, 0:ntok], start=True, stop=True),
                            reads=[b_qb, b_const], writes=[b_ps[rbk]])
                      kb.op(dve, L("tensor_tensor", out=yc, in0=psf[rbk][:, 0:ntok], in1=sint[:, tok0:tok0 + ntok], op=ALU.mult),
                            reads=[b_ps[rbk], b_const], writes=[b_ycvs[cp]])
                      kb.op(dve, L("tensor_tensor", out=dst[:, tok0:tok0 + ntok], in0=pr, in1=yc, op=ALU.add),
                            reads=[b_pres[cp], b_ycvs[cp]], writes=[b_dst])
              cut(5)
              for b in range(NB):
                  proj_tm([2], b, 2 + (b % 2))
                  kb.op(act, L("copy", out=vaug[:, b, 0:128], in_=psf[2 + (b % 2)][:, 0:128]),
                        reads=[b_ps[2 + (b % 2)]], writes=[b_vaug[b]])
                  kb.op(dve, L("tensor_copy", out=vaug[:, b, 128:129], in_=vct[:, b:b + 1]),
                        reads=[b_const], writes=[b_vaug[b]])
              if s == 0 and h == 0:
                  tap("qT0", qT, [b_qT])
                  tap("kT0", kT, [b_kT])
              if h == 0: cut(3)
              a1buf = carve(o_sg, 512)
              att = carve(o_sg + 512, 128)
              junk = carve(o_sg + 640, 128)
              for g in range(4):
                  first = 1 + 4 * g
                  tq0 = 128 * first
                  ei = 0
                  for c in range(2):
                      for j in range(first + 4):
                          r = j - first
                          c0 = max(r, 0) * 128
                          bk = 0 + (ei % 2)
                          eb = ebuf[ei % 2]
                          b_eb = b_e[ei % 2]
                          ei += 1
                          lk = kT[c * 64:(c + 1) * 64, j * 128:(j + 1) * 128]
                          if r >= 0:
                              kb.op(pe, L("matmul",
                                  psf[bk][:, c0:c0 + 128], lhsT=lk, rhs=qT[c * 64:(c + 1) * 64, tq0 + c0:tq0 + c0 + 128],
                                  start=True, stop=False), reads=[b_kT, b_qT], writes=[b_ps[bk]])
                              kb.op(pe, L("matmul", psf[bk][:, c0:c0 + 128], lhsT=identb, rhs=trib,
                                                                         start=False, stop=True),
                                    reads=[b_const], writes=[b_ps[bk]])
                              c1 = c0 + 128
                          else:
                              c1 = c0
                          if c1 < 512:
                              kb.op(pe, L("matmul",
                                  psf[bk][:, c1:512], lhsT=lk, rhs=qT[c * 64:(c + 1) * 64, tq0 + c1:tq0 + 512],
                                  start=True, stop=True), reads=[b_kT, b_qT], writes=[b_ps[bk]])
                          kb.op(act, L("activation", out=eb[:, c0:512], in_=psf[bk][:, c0:512],
                                                                                 func=AF.Exp, scale=0.125),
                                reads=[b_ps[bk]], writes=[b_eb])
                          for il in range(max(r, 0), 4):
                              i = first + il
                              ob = 4 + il
                              kb.op(pe, L("matmul",
                                  psf[ob][:, 0:129], lhsT=eb[:, il * 128:(il + 1) * 128],
                                  rhs=vaug[:, j, 0:129], start=(j == 0), stop=(j == i)),
                                  reads=[b_eb, b_vaug[j]], writes=[b_ps[ob]])
                      for il in range(4):
                          i = first + il
                          ob = 4 + il
                          Oc = psf[ob][:, 0:128]
                          a1 = a1buf[:, il * 128:(il + 1) * 128]
                          kb.op(dve, L("reciprocal", out=small[:, c:c + 1], in_=psf[ob][:, 128:129]),
                                reads=[b_ps[ob]], writes=[b_small])
                          if c == 0:
                              kb.op(act, L("activation", out=a1, in_=Oc, func=AF.Copy, scale=small[:, 0:1]),
                                    reads=[b_ps[ob], b_small], writes=[b_dm])
                              continue
                          kb.op(dve, L("tensor_tensor", out=small[:, 2:3], in0=small[:, 1:2], in1=neglam, op=ALU.mult),
                                reads=[b_small, b_const], writes=[b_small])
                          kb.op(dve, L("scalar_tensor_tensor", out=att, in0=Oc, scalar=small[:, 2:3],
                                                                                   in1=a1, op0=ALU.mult, op1=ALU.add),
                                reads=[b_ps[ob], b_small, b_dm], writes=[b_dm])
                          kb.op(act, L("activation", out=junk, in_=att, func=AF.Square, accum_out=small[:, 3:4]),
                                reads=[b_dm], writes=[b_dm, b_small])
                          kb.op(act, L("activation", out=small[:, 4:5], in_=small[:, 3:4], func=AF.Sqrt,
                                                            scale=1.0 / 128, bias=EPS), reads=[b_small], writes=[b_small])
                          kb.op(dve, L("reciprocal", out=small[:, 5:6], in_=small[:, 4:5]), reads=[b_small],
                                writes=[b_small])
                          kb.op(dve, L("scalar_tensor_tensor", out=mixst[:, 0:128], in0=att, scalar=small[:, 5:6],
                                                                      in1=subln8, op0=ALU.mult, op1=ALU.mult),
                                reads=[b_dm, b_small, b_const], writes=[b_mixst])
                          kb.dma(sp, L("dma_start",
                              out=mixd[s, (i - 1) * 128:i * 128, h * 128:(h + 1) * 128], in_=mixst[:, 0:128]),
                              reads=[b_mixst], writes=[], owner=b_mixst)
          if s == 0 and stage == 2:
              tap("mix", mixd[0][:, 0:1024], [])
          if stage < 3:
              continue
          kb.barrier()
          LI = carve(o_qk, 2176, parts=4)
          EX = carve(o_va, 2176, parts=4)
          Bt = carve(o_pre, 2176, parts=4)
          Ct = carve(o_ycv, 2176, parts=4)
          ONES4 = carve(o_crow, 2176, parts=4)
          b_g = B("gates")
          kb.dma(pool, L("dma_start", out=wg, in_=w_in[:, C_MI:C_MI + 8].rearrange("(k p) c -> p k c", p=128)),
                 writes=[b_wg], owner=b_wg)
          for (tok0, ntok) in TG:
              proj_fm(None, 4, tok0, ntok, 0, lhs=lambda kc: wg[:, kc, 0:4], b_lhs=b_wg)
              kb.op(act, L("activation", out=LI[:, tok0:tok0 + ntok], in_=psf[0][0:4, 0:ntok], func=AF.Identity,
                           bias=gbt[:, 0:1]), reads=[b_ps[0], b_const], writes=[b_g])
              proj_fm(None, 4, tok0, ntok, 1, lhs=lambda kc: wg[:, kc, 4:8], b_lhs=b_wg)
              kb.op(act, L("activation", out=EX[:, tok0:tok0 + ntok], in_=psf[1][0:4, 0:ntok], func=AF.Exp, scale=-1.0,
                           bias=nfb[:, 0:1]), reads=[b_ps[1], b_const], writes=[b_g])
          G = [b_g]
          kb.op(act, L("activation", out=EX, in_=EX, func=AF.Ln, bias=1.0), reads=G, writes=G)
          kb.op(dve, L("tensor_scalar", out=EX, in0=EX, scalar1=-1.0, scalar2=None, op0=ALU.mult), reads=G, writes=G)
          kb.op(dve, L("memset", EX[:, 0:112], 0.0), writes=G)
          kb.op(dve, L("memset", LI[:, 0:112], NEGBIG), writes=G)
          kb.op(dve, L("memset", ONES4, 1.0), writes=G)
          kb.op(dve, L("tensor_tensor_scan", out=Bt, data0=ONES4, data1=EX, initial=0.0, op0=ALU.mult, op1=ALU.add),
                reads=G, writes=G)
          kb.op(dve, L("tensor_tensor", out=Ct, in0=LI, in1=Bt, op=ALU.subtract), reads=G, writes=G)
          Mt = LI
          kb.op(dve, L("tensor_tensor_scan", out=Mt, data0=Ct, data1=Ct, initial=-3.0e38, op0=ALU.max, op1=ALU.max),
                reads=G, writes=G)
          NEGM = EX
          kb.op(dve, L("tensor_scalar", out=NEGM, in0=Mt, scalar1=-1.0, scalar2=None, op0=ALU.mult), reads=G, writes=G)
          kb.op(dve, L("tensor_tensor", out=Bt, in0=Bt, in1=Mt, op=ALU.add), reads=G, writes=G)
          kb.op(dve, L("tensor_scalar", out=Bt, in0=Bt, scalar1=-80.0, scalar2=None, op0=ALU.max), reads=G, writes=G)
          kb.op(act, L("activation", out=Bt, in_=Bt, func=AF.Exp, scale=-1.0), reads=G, writes=G)
          for i in range(1, NB):
              kb.op(pe, L("matmul", psf[2][:, (i - 1) * 4:i * 4], lhsT=Bt[:, i * 128:(i + 1) * 128], rhs=identf[0:4, 0:4],
                          start=True, stop=True), reads=G + [b_const], writes=[b_ps[2]])
          kb.op(act, L("copy", out=emtT, in_=psf[2][:, 0:64]), reads=[b_ps[2]], writes=[b_emtT])
          b_gsc = B("gsc")
          kb.dma(sp, L("dma_start", out=gsc[s, 0], in_=Ct), reads=G, writes=[b_gsc], owner=b_g)
          kb.dma(sp, L("dma_start", out=gsc[s, 1], in_=NEGM), reads=G, writes=[b_gsc], owner=b_g)
          if s == 0:
              tap("emtT", emtT, [b_emtT])
          kb.barrier()
          kb.op(dve, L("memset", pre[:, 0:3], 0.0), writes=[b_pre])
          kb.op(dve, L("memset", crow, 1.0), writes=[b_crow])
          kb.op(dve, L("memset", nmrow, 1.0), writes=[b_nmrow])
          for b in range(NB):
              kb.op(dve, L("memset", vaug[:, b, 256:257], 1.0), writes=[b_vaug[b]])
          hmbuf = carve(o_sg, 256)
          hjunk = carve(o_sg + 256, 256)
          sgbuf = carve(o_sg + 512, 256)
          b_hm = B("hm")
          for h in range(4):
              load_slab(0, C_MQ + h * 128)
              load_slab(1, C_MK + h * 128)
              kb.dma(sp, L("dma_start", out=crow[0:1, :], in_=gsc[s, 0, h:h + 1, :]), reads=[b_gsc], writes=[b_crow],
                     owner=b_crow)
              kb.dma(sp, L("dma_start", out=nmrow[1:2, :], in_=gsc[s, 1, h:h + 1, :]), reads=[b_gsc], writes=[b_nmrow],
                     owner=b_nmrow)
              for which in range(2):
                  dst, b_dst = (qT, b_qT) if which == 0 else (kT, b_kT)
                  ch = which * 4 + h
                  for (tok0, ntok) in TG:
                      proj_fm(which, 128, tok0, ntok, which)
                      kb.op(act, L("copy", out=pre[:, 3 + tok0:3 + tok0 + ntok], in_=psf[which][:, 0:ntok]),
                            reads=[b_ps[which]], writes=[b_pre])
                  kb.op(dve, L("tensor_scalar", out=ycv, in0=pre[:, 0:T], scalar1=cwt[:, ch, 0:1], scalar2=cbt[:, ch:ch + 1],
                               op0=ALU.mult, op1=ALU.add), reads=[b_pre, b_const], writes=[b_ycv])
                  for j in range(1, 4):
                      kb.op(dve, L("scalar_tensor_tensor", out=ycv, in0=pre[:, j:T + j], scalar=cwt[:, ch, j:j + 1], in1=ycv,
                                   op0=ALU.mult, op1=ALU.add), reads=[b_pre, b_ycv, b_const], writes=[b_ycv])
                  kb.op(act, L("activation", out=dst, in_=ycv, func=AF.Silu), reads=[b_ycv], writes=[b_dst])
              load_slab256(1, C_MV + h * 256)
              for b in range(NB):
                  proj_tm256(1, b, 2 + (b % 2))
                  kb.op(act, L("copy", out=vaug[:, b, 0:256], in_=psf[2 + (b % 2)][:, 0:256]), reads=[b_ps[2 + (b % 2)]],
                        writes=[b_vaug[b]])
              load_slab256(0, C_MO + h * 256)
              if s == 0 and h == 0:
                  tap("mq0", qT, [b_qT])
                  tap("mk0", kT, [b_kT])
              dmbs = [dmb, carve(o_sg + 1024, 512)]
              b_dms = [b_dm, B("dm2")]
              stp = 0
              for g in range(4):
                  first = 1 + 4 * g
                  tq0 = 128 * first
                  for j in range(first + 4):
                      r = j - first
                      c0 = max(r, 0) * 128
                      par = stp % 2
                      stp += 1
                      sbk, xbk = (0, 1) if par == 0 else (2, 3)
                      dmc, b_dmc = dmbs[par], b_dms[par]
                      ab = ebuf[par]
                      b_ab = b_e[par]
                      jb = slice(j * 128, (j + 1) * 128)
                      kb.op(pe, L("matmul", psf[sbk][:, c0:512], lhsT=kT[:, jb], rhs=qT[:, tq0 + c0:tq0 + 512], start=True,
                                  stop=True), reads=[b_kT, b_qT], writes=[b_ps[sbk]])
                      if r >= 0:
                          kb.op(pe, L("matmul", psf[xbk][:, c0:c0 + 128], lhsT=crow[:, jb],
                                      rhs=nmrow[:, tq0 + c0:tq0 + c0 + 128], start=True, stop=False),
                                reads=[b_crow, b_nmrow], writes=[b_ps[xbk]], inc=False)
                          kb.op(pe, L("matmul", psf[xbk][:, c0:c0 + 128], lhsT=identb, rhs=trib, start=False, stop=True),
                                reads=[b_const], writes=[b_ps[xbk]])
                          c1 = c0 + 128
                      else:
                          c1 = c0
                      if c1 < 512:
                          kb.op(pe, L("matmul", psf[xbk][:, c1:512], lhsT=crow[:, jb], rhs=nmrow[:, tq0 + c1:tq0 + 512],
                                      start=True, stop=True), reads=[b_crow, b_nmrow], writes=[b_ps[xbk]])
                      kb.op(act, L("activation", out=dmc[:, c0:512], in_=psf[xbk][:, c0:512], func=AF.Exp),
                            reads=[b_ps[xbk]], writes=[b_dmc])
                      kb.op(dve, L("scalar_tensor_tensor", out=ab[:, c0:512], in0=psf[sbk][:, c0:512], scalar=128.0 ** -0.5,
                                   in1=dmc[:, c0:512], op0=ALU.mult, op1=ALU.mult), reads=[b_ps[sbk], b_dmc], writes=[b_ab])
                      for il in range(max(r, 0), 4):
                          kb.op(pe, L("matmul", psf[4 + il][:, 0:257], lhsT=ab[:, il * 128:(il + 1) * 128],
                                      rhs=vaug[:, j, 0:257], start=(j == 0), stop=(j == first + il)),
                                reads=[b_ab, b_vaug[j]], writes=[b_ps[4 + il]], inc=(il == 3))
                  for il in range(4):
                      i = first + il
                      ob = 4 + il
                      e0 = (i - 1) * 4 + h
                      kb.op(act, L("activation", out=small[:, 0:1], in_=psf[ob][:, 256:257], func=AF.Abs),
                            reads=[b_ps[ob]], writes=[b_small])
                      kb.op(dve, L("tensor_tensor", out=small[:, 1:2], in0=small[:, 0:1], in1=emtT[:, e0:e0 + 1], op=ALU.max),
                            reads=[b_small, b_emtT], writes=[b_small])
                      kb.op(dve, L("reciprocal", out=small[:, 2:3], in_=small[:, 1:2]), reads=[b_small], writes=[b_small])
                      kb.op(act, L("activation", out=hmbuf, in_=psf[ob][:, 0:256], func=AF.Copy, scale=small[:, 2:3]),
                            reads=[b_ps[ob], b_small], writes=[b_hm])
                      kb.op(act, L("activation", out=hjunk, in_=hmbuf, func=AF.Square, accum_out=small[:, 3:4]),
                            reads=[b_hm], writes=[b_hm, b_small])
                      kb.op(act, L("activation", out=small[:, 4:5], in_=small[:, 3:4], func=AF.Sqrt, scale=1.0 / 256,
                                   bias=EPS), reads=[b_small], writes=[b_small])
                      kb.op(dve, L("reciprocal", out=small[:, 5:6], in_=small[:, 4:5]), reads=[b_small], writes=[b_small])
                      proj_tm256(0, i, 2)
                      kb.op(act, L("activation", out=sgbuf, in_=psf[2][:, 0:256], func=AF.Sigmoid), reads=[b_ps[2]],
                            writes=[b_hm])
                      kb.op(dve, L("scalar_tensor_tensor", out=hjunk, in0=hmbuf, scalar=small[:, 5:6],
                                   in1=mnwt[:, h * 256:(h + 1) * 256], op0=ALU.mult, op1=ALU.mult),
                            reads=[b_hm, b_small, b_const], writes=[b_hm])
                      kb.op(dve, L("tensor_tensor", out=mixst[:, 0:256], in0=hjunk, in1=sgbuf, op=ALU.mult), reads=[b_hm],
                            writes=[b_mixst])
                      kb.dma(sp, L("dma_start", out=mixd[s, (i - 1) * 128:i * 128, 1024 + h * 256:1024 + (h + 1) * 256],
                                   in_=mixst[:, 0:256]), reads=[b_mixst], writes=[], owner=b_mixst)
          if s == 0:
              tap("mix", mixd[0], [])
          if stage < 4:
              continue
          kb.barrier()
          wbig = carve(o_big, 16384, BF16, "p (k c) -> p k c", k=KC)
          b_wb = [B("wbig%d" % n) for n in range(4)]

          def load_wbig(src):
              for n in range(4):
                  kb.dma(pool, L("dma_start", out=wbig[:, :, n * 512:(n + 1) * 512],
                                 in_=src[:, n * 512:(n + 1) * 512].rearrange("(k p) c -> p k c", p=128)),
                         writes=[b_wb[n]], owner=b_wb[n])
          load_wbig(w_out)
          for blk in range(1, NB):
              r0 = (blk - 1) * 128
              kb.dma(sp, L("dma_start", out=xb, in_=mixd[s, r0:r0 + 128, :]), writes=[b_xb], owner=b_xb)
              kb.dma(sp, L("dma_start", out=xt[0], in_=xp[s, blk * 128:(blk + 1) * 128, :]), writes=[b_xt[0]], owner=b_xt[0])
              transpose16(xb, b_xb, lambda g: tT[:, g * 4:(g + 1) * 4, :], b_tT, bank0=0)
              for n in range(4):
                  for kc in range(KC):
                      kb.op(pe, L("matmul", psf[4 + n][:, 0:512], lhsT=tT[:, kc, :], rhs=wbig[:, kc, n * 512:(n + 1) * 512],
                                  start=(kc == 0), stop=(kc == KC - 1)), reads=[b_tT, b_wb[n]], writes=[b_ps[4 + n]],
                        inc=(kc == KC - 1))
                  kb.op(dve, L("tensor_tensor", out=xt[1][:, n * 512:(n + 1) * 512], in0=psf[4 + n][:, 0:512],
                               in1=xt[0][:, n * 512:(n + 1) * 512], op=ALU.add), reads=[b_ps[4 + n], b_xt[0]],
                        writes=[b_xt[1]])
              kb.dma(sp, L("dma_start", out=xmidd[s, r0:r0 + 128, :], in_=xt[1]), reads=[b_xt[1]], writes=[], owner=b_xt[1])
          if s == 0:
              tap("xmid", xmidd[0], [])
          if stage < 5:
              continue
          o_r1 = o_crow
          skb = carve(o_big + 16384, 1024, BF16, "p (c n) -> p c n", c=16)
          qTc = carve(o_r1, 1024, BF16, "p (c t) -> p c t", c=16)
          sc = carve(o_r1 + 1024, 2048, F32, "p (c n) -> p c n", c=16)
          sc2 = carve(o_r1 + 3072, 2048, F32, "p (c n) -> p c n", c=16)
          cand_s = carve(o_r1 + 5120, 2048, F32, "p (h n) -> p h n", h=8)
          cand_i = carve(o_r1 + 1024, 2048, F32, "p (h n) -> p h n", h=8)
          cand2 = carve(o_r1 + 3072, 2048, F32, "p (h n) -> p h n", h=8)
          osm = o_r1 + 7168
          top = carve(osm, 256, F32, "p (c k) -> p c k", c=16); osm += 256
          ti = carve(osm, 256, U32, "p (c k) -> p c k", c=16); osm += 256
          tif = carve(osm, 256, F32, "p (c k) -> p c k", c=16); osm += 256
          junkc = carve(osm, 256); osm += 256
          best = carve(osm, 128, F32, "p (h k) -> p h k", h=8); osm += 128
          idxf = carve(osm, 128); osm += 128
          exb = carve(osm, 128, F32, "p (h k) -> p h k", h=8); osm += 128
          smb = carve(osm, 16); osm += 16
          posu = carve(osm, 128, U32, "p (h k) -> p h k", h=8); osm += 128
          posf = carve(osm, 128, F32, "p (h k) -> p h k", h=8); osm += 128
          paf = carve(osm, 128, F32, "p (h k) -> p h k", h=8); osm += 128
          pbf = carve(osm, 128, F32, "p (h k) -> p h k", h=8); osm += 128
          i2sel = carve(osm, 128); osm += 128
          assert osm <= o_r1 + 9728
          eqb = carve(o_r1 + 1024, 2048)
          b_pos = B("p4pos")
          idx_all = carve(o_sin, 2048, I32, "p (b k) -> p b k", b=16)
          gate_all = carve(o_cos, 2048, F32, "p (b k) -> p b k", b=16)
          o_r3 = o_p1 + 9216
          actr = carve(o_r3, 128)
          gab = carve(o_r3 + 128, 128)
          coef = carve(o_r3 + 256, 128)
          b_sc, b_sc2, b_cs_, b_top, b_ti, b_best, b_idxf, b_ex = [B("p4_%d" % q) for q in range(8)]
          b_qTc, b_skb, b_idx, b_gate, b_y, b_actr = [B("p4b_%d" % q) for q in range(6)]
          for half in range(1):
              kb.barrier()
              load_wbig(w_q)
              kb.dma(pool, L("dma_start", out=skb, in_=skT), writes=[b_skb], owner=b_skb)
              kb.dma(sp, L("dma_start", out=nrm, in_=nrm3[1]), writes=[b_nrm], owner=b_nrm)
              kb.dma(sp, L("dma_start", out=nrm2, in_=nrm3[2]), writes=[b_nrm2], owner=b_nrm2)
              for bl in range(16):
                  blk = 1 + bl
                  r0 = (blk - 1) * 128
                  x = xt[bl % 2]
                  bx = b_xt[bl % 2]
                  kb.dma(sp, L("dma_start", out=x, in_=xmidd[s, r0:r0 + 128, :]), writes=[bx], owner=bx)
                  rmsnorm_to(x, bx, nrm, b_nrm, xb, b_xb)
                  transpose16(xb, b_xb, lambda g: tT[:, g * 4:(g + 1) * 4, :], b_tT, bank0=0)
                  for n in range(4):
                      for kc in range(KC):
                          kb.op(pe, L("matmul", psf[4 + n][:, 0:512], lhsT=tT[:, kc, :], rhs=wbig[:, kc, n * 512:(n + 1) * 512],
                                      start=(kc == 0), stop=(kc == KC - 1)), reads=[b_tT, b_wb[n]], writes=[b_ps[4 + n]],
                                inc=(kc == KC - 1))
                      kb.op(act, L("copy", out=xb[:, n * 512:(n + 1) * 512], in_=psf[4 + n][:, 0:512]), reads=[b_ps[4 + n]],
                            writes=[b_xb])
                  transpose16(xb, b_xb, lambda g: qTc[:, g * 4:(g + 1) * 4, :], b_qTc, bank0=2)
                  for cg in range(4):
                      bk = 4 + (cg % 2)
                      for cc in range(4):
                          c = cg * 4 + cc
                          kb.op(pe, L("matmul", psf[bk][:, cc * 128:(cc + 1) * 128], lhsT=qTc[:, c, :], rhs=skb[:, c, :],
                                      start=True, stop=True), reads=[b_qTc, b_skb], writes=[b_ps[bk]], inc=(cc == 3))
                      kb.op(act, L("copy", out=sc[:, cg * 4:(cg + 1) * 4, :],
                                   in_=psf[bk][:, 0:512].rearrange("p (c t) -> p c t", c=4)), reads=[b_ps[bk]], writes=[b_sc])
                  for c in range(16):
                      kb.op(dve, L("max", out=top[:, c, 0:8], in_=sc[:, c, :]), reads=[b_sc], writes=[b_top])
                      kb.op(dve, L("max_index", out=ti[:, c, 0:8], in_max=top[:, c, 0:8], in_values=sc[:, c, :]),
                            reads=[b_sc, b_top], writes=[b_ti])
                      kb.op(dve, L("match_replace", out=sc2[:, c, :], in_to_replace=top[:, c, 0:8], in_values=sc[:, c, :],
                                   imm_value=NEGBIG), reads=[b_sc, b_top], writes=[b_sc2])
                      kb.op(dve, L("max", out=top[:, c, 8:16], in_=sc2[:, c, :]), reads=[b_sc2], writes=[b_top])
                      kb.op(dve, L("max_index", out=ti[:, c, 8:16], in_max=top[:, c, 8:16], in_values=sc2[:, c, :]),
                            reads=[b_sc2, b_top], writes=[b_ti])
                  kb.op(dve, L("tensor_copy", out=tif, in_=ti), reads=[b_ti], writes=[b_ti])
                  top4 = top.rearrange("p (h two) k -> p h two k", two=2)
                  tif4 = tif.rearrange("p (h two) k -> p h two k", two=2)
                  kb.op(dve, L("tensor_scalar", out=tif4[:, :, 0, :], in0=tif4[:, :, 0, :], scalar1=128.0, scalar2=None,
                               op0=ALU.mult), reads=[b_ti], writes=[b_ti])
                  cs4 = cand_s.rearrange("p h (a b) -> p h a b", a=16)
                  ci4 = cand_i.rearrange("p h (a b) -> p h a b", a=16)
                  bc = [128, 8, 16, 16]
                  kb.op(dve, L("tensor_tensor", out=cs4, in0=top4[:, :, 0, :].unsqueeze(3).broadcast_to(bc),
                               in1=top4[:, :, 1, :].unsqueeze(2).broadcast_to(bc), op=ALU.add), reads=[b_top], writes=[b_cs_])
                  for h in range(8):
                      kb.op(dve, L("max", out=best[:, h, 0:8], in_=cand_s[:, h, :]), reads=[b_cs_], writes=[b_best])
                      kb.op(dve, L("max_index", out=posu[:, h, 0:8], in_max=best[:, h, 0:8], in_values=cand_s[:, h, :]),
                            reads=[b_cs_, b_best], writes=[b_pos])
                      kb.op(dve, L("match_replace", out=cand2[:, h, :], in_to_replace=best[:, h, 0:8],
                                   in_values=cand_s[:, h, :], imm_value=NEGBIG), reads=[b_cs_, b_best], writes=[b_sc2])
                      kb.op(dve, L("max", out=best[:, h, 8:16], in_=cand2[:, h, :]), reads=[b_sc2], writes=[b_best])
                      kb.op(dve, L("max_index", out=posu[:, h, 8:16], in_max=best[:, h, 8:16], in_values=cand2[:, h, :]),
                            reads=[b_sc2, b_best], writes=[b_pos])
                  P = [b_pos]
                  eq4 = eqb.rearrange("p (h k a) -> p h k a", h=8, k=16)
                  eq3 = eqb.rearrange("p (n a) -> p n a", a=16)
                  kb.op(dve, L("tensor_copy", out=posf, in_=posu), reads=P, writes=P)
                  kb.op(dve, L("tensor_tensor", out=eq4, in0=posf.unsqueeze(3).broadcast_to(bc),
                               in1=lo16.unsqueeze(1).unsqueeze(1).broadcast_to(bc), op=ALU.is_ge), reads=P + [b_const],
                        writes=[b_sc])
                  kb.op(dve, L("tensor_reduce", out=paf.rearrange("p h k -> p (h k)"), in_=eq3, axis=AX.X, op=ALU.add),
                        reads=[b_sc], writes=P)
                  kb.op(dve, L("tensor_scalar", out=paf, in0=paf, scalar1=-1.0, scalar2=None, op0=ALU.add), reads=P, writes=P)
                  kb.op(dve, L("scalar_tensor_tensor", out=pbf, in0=paf, scalar=-16.0, in1=posf, op0=ALU.mult, op1=ALU.add),
                        reads=P, writes=P)
                  io4 = iota16.unsqueeze(1).unsqueeze(1).broadcast_to(bc)
                  for (pf, side, dsel) in ((paf, 0, idxf), (pbf, 1, i2sel)):
                      kb.op(dve, L("tensor_tensor", out=eq4, in0=pf.unsqueeze(3).broadcast_to(bc), in1=io4, op=ALU.is_equal),
                            reads=P + [b_const], writes=[b_sc])
                      kb.op(dve, L("tensor_tensor", out=eq4, in0=eq4, in1=tif4[:, :, side, :].unsqueeze(2).broadcast_to(bc),
                                   op=ALU.mult), reads=[b_sc, b_ti], writes=[b_sc])
                      kb.op(dve, L("tensor_reduce", out=dsel, in_=eq3, axis=AX.X, op=ALU.add), reads=[b_sc], writes=[b_idxf])
                  kb.op(dve, L("tensor_tensor", out=idxf, in0=idxf, in1=i2sel, op=ALU.add), reads=[b_idxf], writes=[b_idxf])
                  kb.op(dve, L("tensor_scalar", out=idxf, in0=idxf, scalar1=16383.0, scalar2=None, op0=ALU.min),
                        reads=[b_idxf], writes=[b_idxf])
                  kb.op(dve, L("tensor_copy", out=idx_all[:, bl, :], in_=idxf), reads=[b_idxf], writes=[b_idx])
                  kb.op(dve, L("tensor_tensor", out=exb, in0=best, in1=best[:, :, 0:1].broadcast_to([128, 8, 16]),
                               op=ALU.subtract), reads=[b_best], writes=[b_ex])
                  kb.op(act, L("activation", out=exb, in_=exb, func=AF.Exp), reads=[b_ex], writes=[b_ex])
                  kb.op(dve, L("tensor_reduce", out=smb[:, 0:8], in_=exb, axis=AX.X, op=ALU.add), reads=[b_ex], writes=[b_ex])
                  kb.op(dve, L("reciprocal", out=smb[:, 8:16], in_=smb[:, 0:8]), reads=[b_ex], writes=[b_ex])
                  kb.op(dve, L("tensor_tensor", out=gate_all[:, bl, :].rearrange("p (h k) -> p h k", h=8), in0=exb,
                               in1=smb[:, 8:16].unsqueeze(2).broadcast_to([128, 8, 16]), op=ALU.mult), reads=[b_ex],
                        writes=[b_gate])
              if s == 0 and half == 0:
                  tap("idx", idx_all[:, 0:8, :].rearrange("p b k -> p (b k)"), [b_idx])
                  tap("gate", gate_all[:, 0:8, :].rearrange("p b k -> p (b k)"), [b_gate])
              if stage < 6:
                  continue
              kb.barrier()
              gbuf = [carve(o_big + 2048 * q, 2048, BF16) for q in range(8)]
              b_gb = [B("gb%d" % q) for q in range(8)]
              hnb = [carve(o_r1, 2048), carve(o_r1 + 2048, 2048)]
              b_hn = [B("hn%d" % q) for q in range(2)]
              zbuf = carve(o_r1 + 4096, 2048)
              outt = carve(o_r1 + 6144, 2048)
              b_z, b_outt = B("zb"), B("outt")
              dg = [carve(o_r1 + 8192 + 64 * q, 64, BF16) for q in range(4)]
              b_dg = [B("dg%d" % q) for q in range(4)]
              arb = [carve(o_r3, 128), carve(o_r3 + 128, 128)]
              glb = [carve(o_r3 + 256, 128), carve(o_r3 + 384, 128)]
              b_ar = [B("ar%d" % q) for q in range(8)]
              b_gl = [B("gl%d" % q) for q in range(8)]
              for bl in range(16):
                  pu = bl % 2
                  ru = bl * 128
                  kb.dma(sp, L("dma_start", out=xt[pu], in_=xmidd[s, ru:ru + 128, :]), writes=[b_xt[pu]], owner=b_xt[pu])
                  rmsnorm_to(xt[pu], b_xt[pu], nrm, b_nrm, hnb[pu], b_hn[pu])

                  def vstep(k):
                      q = k % 8
                      d4 = k % 4
                      kb.op(dve, L("tensor_scalar", out=dg[d4], in0=identb, scalar1=glb[pu][:, k:k + 1],
                                   scalar2=gate_all[:, bl, k:k + 1], op0=ALU.mult, op1=ALU.mult),
                            reads=[b_const, b_gl[q], b_gate], writes=[b_dg[d4]])
                      for n in range(4):
                          kb.op(pe, L("matmul", psf[4 + n][:, 0:512], lhsT=dg[d4],
                                      rhs=gbuf[q][:, D + n * 512:D + (n + 1) * 512], start=(k == 0), stop=(k == 127)),
                                reads=[b_dg[d4], b_gb[q]], writes=[b_ps[4 + n]], inc=(n == 3))

                  for k in range(128):
                      q = k % 8
                      kb.dma(pool, L("indirect_dma_start", out=gbuf[q], out_offset=None, in_=uvd,
                                     in_offset=bass.IndirectOffsetOnAxis(ap=idx_all[:, bl, k:k + 1], axis=0)),
                             reads=[b_idx], writes=[b_gb[q]], owner=b_gb[q])
                      kb.op(dve, L("scalar_tensor_tensor", out=xb, in0=hnb[pu], scalar=1.0, in1=gbuf[q][:, 0:D], op0=ALU.mult,
                                   op1=ALU.mult, accum_out=arb[pu][:, k:k + 1]), reads=[b_hn[pu], b_gb[q]],
                            writes=[b_xb, b_ar[q]])
                      kb.op(act, L("activation", out=glb[pu][:, k:k + 1], in_=arb[pu][:, k:k + 1], func=AF.Gelu),
                            reads=[b_ar[q]], writes=[b_gl[q]])
                      if k >= 1:
                          vstep(k - 1)
                  vstep(127)
                  for n in range(4):
                      kb.op(dve, L("tensor_tensor", out=zbuf[:, n * 512:(n + 1) * 512], in0=psf[4 + n][:, 0:512],
                                   in1=xt[pu][:, n * 512:(n + 1) * 512], op=ALU.add), reads=[b_ps[4 + n], b_xt[pu]],
                            writes=[b_z])
                  rmsnorm_to(zbuf, b_z, nrm2, b_nrm2, outt, b_outt)
                  kb.dma(sp, L("dma_start", out=out[s, ru:ru + 128, :], in_=outt), reads=[b_outt], writes=[],
                         owner=b_outt)
      except StopBuild:
        break

    kb.barrier()
    kb.emit()
    return nc


def host_consts():
    pos = np.maximum(np.arange(T) - 112, 0).astype(np.float32)
    inv = (np.float32(10000.0) ** (-np.arange(0, 64, 2, dtype=np.float32) / np.float32(64))).astype(np.float32)
    ang = pos[:, None] * inv[None, :]
    ang = np.concatenate([ang, ang], axis=-1)
    ang = np.concatenate([ang, ang], axis=-1).T
    cosd = np.cos(ang).astype(np.float32)
    sind = np.sin(ang).astype(np.float32)
    cm = np.zeros((4, 128, 128), np.float32)
    cm[0] = np.eye(128, dtype=np.float32)
    for po in range(128):
        base = (po // 64) * 64
        d = po % 64
        if d < 32:
            cm[1, base + d + 32, po] = -1.0
        else:
            cm[1, base + d - 32, po] = 1.0
    kk, qq = np.meshgrid(np.arange(128), np.arange(128), indexing="ij")
    cm[2] = np.where(kk > qq, -30000.0, 0.0)
    cm[3] = np.arange(128, dtype=np.float32)[None, :]
    cm[3][:, 16:32] = 16.0 * np.arange(16, dtype=np.float32)[None, :]
    validc = np.ones((128, NB), np.float32)
    validc[0:112, 0] = 0.0
    return dict(cosd=np.ascontiguousarray(cosd), sind=np.ascontiguousarray(sind), cmat=cm, validc=validc)


def host_layout(inp):
    f = lambda a: np.ascontiguousarray(np.asarray(a, dtype=np.float32))
    c = host_consts()
    c["nrm3"] = f(np.stack([np.broadcast_to(np.asarray(inp[k]).reshape(1, D), (128, D))
                            for k in ("norm_mix_w", "norm_ffn_w", "norm_final_w")]))
    c["w_in"] = f(inp["w_in"][0])
    c["w_out"] = f(inp["w_out"][0])
    c["w_q"] = f(inp["peer_w_q"][0])
    c["subln"] = f(np.broadcast_to(np.asarray(inp["attn_subln_w"][0]).reshape(1, 128), (128, 128)))
    c["mnw"] = f(np.broadcast_to(np.asarray(inp["mlstm_norm_w"][0]).reshape(1, 1024), (128, 1024)))
    c["convw"] = f(np.asarray(inp["mlstm_conv_w"][0]).reshape(4, 8, 128).transpose(2, 1, 0))
    c["convb"] = f(np.asarray(inp["mlstm_conv_b"][0]).reshape(8, 128).T)
    c["gbias"] = f(np.stack([np.asarray(inp["mlstm_i_b"][0]), np.asarray(inp["mlstm_f_b"][0])], axis=1))
    c["lamqk"] = f(np.asarray(inp["attn_lambda_qk"][0]).reshape(1, 256))
    c["skT"] = f(np.asarray(inp["peer_sub_keys"][0]).reshape(16, 128, 128).transpose(2, 0, 1))
    return c


def host_xp(inp, core):
    x = np.asarray(inp["x"], dtype=np.float32)
    meta = np.asarray(inp["meta_tokens"], dtype=np.float32)
    xp = np.zeros((NSEQ, T, D), np.float32)
    for s in range(NSEQ):
        xp[s, 112:128] = meta
        xp[s, 128:] = x[core * NSEQ + s]
    return xp


_CACHE = {}


def kernel(**inp):
    if "nc" not in _CACHE:
        _CACHE["nc"] = build()
    nc = _CACHE["nc"]
    c = host_layout(inp)
    c["peer_u"] = np.ascontiguousarray(np.asarray(inp["peer_u"][0], dtype=np.float32))
    c["peer_v"] = np.ascontiguousarray(np.asarray(inp["peer_v"][0], dtype=np.float32))
    in_maps = []
    for core in range(NCORES):
        m = dict(c)
        m["xp"] = host_xp(inp, core)
        in_maps.append(m)
    res = run_bass_kernel_spmd(nc, in_maps, core_ids=list(range(NCORES)))
    outs = [np.asarray(r["out"]).reshape(NSEQ, 2048, D) for r in res.results]
    return np.concatenate(outs, axis=0).astype(np.float32)
```
